# Optimizing a Trainium2 kernel written in Bass

```python
import jax, jax.numpy as jnp
from jax import lax
import numpy as np

D_MODEL = 1024
BATCH = 32
SEQ = 2048
DEPTH = 2

GRID_W = 64
CTX_LEN = 256
RET_HEADS = 4
RET_DK = 128
RET_DV = 128
RET_CHUNK = 128
SWA_HEADS = 8
SWA_KV_HEADS = 2
SWA_HEAD_DIM = 64
SWA_WINDOW = 128
SWA_BLOCK = 128
NA_HEADS = 16
NA_HEAD_DIM = 64
NA_ROWS = 8
NA_COLS = 16
D_FF = 2816
CONV_W = 3
ROPE_BASE = 10000.0
EPS = 1e-6
NEG_INF = -1e30
RET_Q = RET_HEADS * RET_DK
RET_V = RET_HEADS * RET_DV
SWA_Q = SWA_HEADS * SWA_HEAD_DIM
SWA_KV = SWA_KV_HEADS * SWA_HEAD_DIM
AB_SIZES = (RET_Q, RET_Q, RET_V, RET_V, SWA_Q, SWA_KV, SWA_KV)
AB_IN = 2 * RET_Q + 2 * RET_V + SWA_Q + 2 * SWA_KV
AB_MIX = RET_V + SWA_Q
NA_DIM = NA_HEADS * NA_HEAD_DIM
N_EVEN = (DEPTH + 1) // 2
N_ODD = DEPTH // 2

kernel_name = 'hybrid_retention_swa_natten_dit'


def rmsnorm(x, g):
    xf = x.astype(jnp.float32)
    y = xf * lax.rsqrt(jnp.mean(xf * xf, axis=-1, keepdims=True) + EPS)
    return (y * g.astype(jnp.float32)).astype(x.dtype)


def modulate(h, shift, scale):
    return h * (1.0 + scale) + shift


def heads(x, n):
    b, t, _ = x.shape
    return x.reshape(b, t, n, -1).transpose(0, 2, 1, 3)


def merge(x):
    b, h, t, d = x.shape
    return x.transpose(0, 2, 1, 3).reshape(b, t, h * d)


def rope_1d(x, pos):
    half = x.shape[-1] // 2
    inv_freq = ROPE_BASE ** (-jnp.arange(half, dtype=jnp.float32) / half)
    ang = pos.astype(jnp.float32)[:, None] * inv_freq[None, :]
    cos, sin = jnp.cos(ang).astype(x.dtype), jnp.sin(ang).astype(x.dtype)
    x1, x2 = x[..., :half], x[..., half:]
    return jnp.concatenate([x1 * cos - x2 * sin, x1 * sin + x2 * cos], axis=-1)


def rope_2d(x):
    t = jnp.arange(x.shape[2])
    half = x.shape[-1] // 2
    return jnp.concatenate([rope_1d(x[..., :half], t // GRID_W), rope_1d(x[..., half:], t % GRID_W)], axis=-1)


def retention_chunked(q, k, v, log_g, s0, strict):
    b, h, t, dk = q.shape
    dv = v.shape[-1]
    n = t // RET_CHUNK

    def chunks(a):
        return a.reshape(b, h, n, RET_CHUNK, a.shape[-1]).transpose(2, 0, 1, 3, 4)

    j = jnp.arange(RET_CHUNK, dtype=jnp.float32)
    diff = j[:, None] - j[None, :]
    mask = diff > 0 if strict else diff >= 0
    dmat = jnp.where(mask, jnp.exp(log_g[:, None, None] * jnp.maximum(diff, 0.0)), 0.0)
    q_dec = jnp.exp(log_g[:, None] * (j + 1.0))[..., None]
    k_dec = jnp.exp(log_g[:, None] * (RET_CHUNK - 1.0 - j))[..., None]
    c_dec = jnp.exp(log_g * RET_CHUNK)[:, None, None]

    def step(s, inp):
        qi, ki, vi = inp
        inner = jnp.einsum('bhid,bhjd->bhij', qi, ki) * dmat
        o = jnp.einsum('bhij,bhjv->bhiv', inner, vi) + jnp.einsum('bhid,bhdv->bhiv', qi * q_dec, s)
        s = s * c_dec + jnp.einsum('bhjd,bhjv->bhdv', ki * k_dec, vi)
        return s, o

    s, o = lax.scan(step, s0, (chunks(q), chunks(k), chunks(v)))
    return o.transpose(1, 2, 0, 3, 4).reshape(b, h, t, dv), s


def retention_final_state(k, v, log_g):
    t = k.shape[2]
    w = jnp.exp(log_g[:, None] * (t - 1.0 - jnp.arange(t, dtype=jnp.float32)))
    return jnp.einsum('bhtd,bhtv->bhdv', k * w[None, :, :, None], v)


def bidir_retention(qx, kx, vx, qc, kc, vc, log_g, need_ctx):
    flip = lambda a: jnp.flip(a, axis=2)
    b, h, _, dk = qx.shape
    zeros = jnp.zeros((b, h, dk, vx.shape[-1]), jnp.float32)
    oc = None
    if need_ctx:
        oc_f, s_f = retention_chunked(qc, kc, vc, log_g[0], zeros, False)
        oc_b, s_b = retention_chunked(flip(qc), flip(kc), flip(vc), log_g[1], zeros, True)
        oc = oc_f + flip(oc_b)
    else:
        s_f = retention_final_state(kc, vc, log_g[0])
        s_b = retention_final_state(flip(kc), flip(vc), log_g[1])
    ox_f, _ = retention_chunked(qx, kx, vx, log_g[0], s_f, False)
    ox_b, _ = retention_chunked(flip(qx), flip(kx), flip(vx), log_g[1], s_b, True)
    return ox_f + flip(ox_b), oc


def ret_out(y, g):
    yn = y * lax.rsqrt(jnp.mean(y * y, axis=-1, keepdims=True) + EPS)
    return merge(yn).astype(g.dtype) * jax.nn.silu(g)


def window_attention(qx, kx, vx, qc, kc, vc, sink, need_ctx):
    b, hq, s, d = qx.shape
    hkv = kx.shape[1]
    g = hq // hkv
    n_ctx = kc.shape[2]
    scale = d ** -0.5
    span = SWA_BLOCK + 2 * SWA_WINDOW
    sink_g = sink.astype(jnp.float32).reshape(1, hkv, g, 1, 1)

    def ctx_logits(q):
        s_ctx = jnp.einsum('bhgqd,bhkd->bhgqk', q, kc).astype(jnp.float32) * scale
        return s_ctx, jnp.broadcast_to(sink_g, s_ctx.shape[:-1] + (1,))

    qg = qx.reshape(b, hkv, g, s, d)
    pad = ((0, 0), (0, 0), (SWA_WINDOW, SWA_WINDOW), (0, 0))
    kp, vp = jnp.pad(kx, pad), jnp.pad(vx, pad)
    band = jnp.abs(jnp.arange(SWA_BLOCK)[:, None] - jnp.arange(span)[None, :] + SWA_WINDOW) <= SWA_WINDOW

    def block(i):
        start = i * SWA_BLOCK
        qb = lax.dynamic_slice_in_dim(qg, start, SWA_BLOCK, axis=3)
        kb = lax.dynamic_slice_in_dim(kp, start, span, axis=2)
        vb = lax.dynamic_slice_in_dim(vp, start, span, axis=2)
        kpos = start - SWA_WINDOW + jnp.arange(span)
        valid = band & ((kpos >= 0) & (kpos < s))[None, :]
        s_loc = jnp.einsum('bhgqd,bhkd->bhgqk', qb, kb).astype(jnp.float32) * scale
        s_loc = jnp.where(valid, s_loc, NEG_INF)
        s_ctx, s_sink = ctx_logits(qb)
        p = jax.nn.softmax(jnp.concatenate([s_loc, s_ctx, s_sink], axis=-1), axis=-1).astype(vx.dtype)
        return (jnp.einsum('bhgqk,bhkd->bhgqd', p[..., :span], vb)
                + jnp.einsum('bhgqk,bhkd->bhgqd', p[..., span:span + n_ctx], vc))

    ox = lax.map(block, jnp.arange(s // SWA_BLOCK))
    ox = ox.transpose(1, 2, 3, 0, 4, 5).reshape(b, hq, s, d)
    oc = None
    if need_ctx:
        qcg = qc.reshape(b, hkv, g, n_ctx, d)
        s_ctx, s_sink = ctx_logits(qcg)
        p = jax.nn.softmax(jnp.concatenate([s_ctx, s_sink], axis=-1), axis=-1).astype(vc.dtype)
        oc = jnp.einsum('bhgqk,bhkd->bhgqd', p[..., :n_ctx], vc).reshape(b, hq, n_ctx, d)
    return ox, oc


def neighbourhood_attention(qx, kx, vx, qc, kc, vc, rpb, need_ctx):
    b, h, s, d = qx.shape
    rows = s // GRID_W
    n_kr = min(NA_ROWS, rows)
    ncb = GRID_W // NA_COLS
    sw = 2 * NA_COLS
    n_ctx = kc.shape[2]
    scale = d ** -0.5
    sup = np.clip(np.arange(ncb) * NA_COLS - NA_COLS // 2, 0, GRID_W - sw)[:, None] + np.arange(sw)[None, :]
    qcol = np.arange(GRID_W).reshape(ncb, NA_COLS)
    qstart = np.clip(qcol - NA_COLS // 2, 0, GRID_W - NA_COLS)
    kcol = sup[:, None, :]
    col_valid = (kcol >= qstart[..., None]) & (kcol < qstart[..., None] + NA_COLS)
    dc = np.clip(kcol - qcol[..., None], 1 - NA_COLS, NA_COLS - 1) + NA_COLS - 1
    rpb_c = rpb.astype(jnp.float32)[:, :, dc]
    qg = qx.reshape(b, h, rows, GRID_W, d)
    kg = kx.reshape(b, h, rows, GRID_W, d)
    vg = vx.reshape(b, h, rows, GRID_W, d)

    def row(r):
        rs = jnp.clip(r - n_kr // 2, 0, rows - n_kr)
        qr = lax.dynamic_index_in_dim(qg, r, axis=2, keepdims=False).reshape(b, h, ncb, NA_COLS, d)
        kr = lax.dynamic_slice_in_dim(kg, rs, n_kr, axis=2)[:, :, :, sup]
        vr = lax.dynamic_slice_in_dim(vg, rs, n_kr, axis=2)[:, :, :, sup]
        bias = jnp.take(rpb_c, rs - r + NA_ROWS - 1 + jnp.arange(n_kr), axis=1).transpose(0, 2, 3, 1, 4)
        s_loc = jnp.einsum('bhmqd,bhrmkd->bhmqrk', qr, kr).astype(jnp.float32) * scale + bias
        s_loc = jnp.where(col_valid[:, :, None, :], s_loc, NEG_INF).reshape(b, h, ncb, NA_COLS, n_kr * sw)
        s_ctx = jnp.einsum('bhmqd,bhnd->bhmqn', qr, kc).astype(jnp.float32) * scale
        p = jax.nn.softmax(jnp.concatenate([s_loc, s_ctx], axis=-1), axis=-1).astype(vx.dtype)
        p_loc = p[..., :n_kr * sw].reshape(b, h, ncb, NA_COLS, n_kr, sw)
        o = (jnp.einsum('bhmqrk,bhrmkd->bhmqd', p_loc, vr)
             + jnp.einsum('bhmqn,bhnd->bhmqd', p[..., n_kr * sw:], vc))
        return o.reshape(b, h, GRID_W, d)

    ox = lax.map(row, jnp.arange(rows)).transpose(1, 2, 0, 3, 4).reshape(b, h, s, d)
    oc = None
    if need_ctx:
        p = jax.nn.softmax(jnp.einsum('bhqd,bhkd->bhqk', qc, kc).astype(jnp.float32) * scale, axis=-1).astype(vc.dtype)
        oc = jnp.einsum('bhqk,bhkd->bhqd', p, vc)
    return ox, oc


def mixer_ab(hx, hc, w_in, w_out, decay_exp, sink, need_ctx):
    idx = np.cumsum(AB_SIZES)[:-1].tolist()
    px = jnp.split(hx @ w_in, idx, axis=-1)
    pc = jnp.split(hc @ w_in, idx, axis=-1)
    f32 = jnp.float32
    t = jnp.arange(hx.shape[1])
    k_scale = RET_DK ** -0.5
    qa_x = rope_1d(heads(px[0], RET_HEADS), t).astype(f32)
    ka_x = rope_1d(heads(px[1], RET_HEADS), t).astype(f32) * k_scale
    va_x = heads(px[2], RET_HEADS).astype(f32)
    qa_c = heads(pc[0], RET_HEADS).astype(f32)
    ka_c = heads(pc[1], RET_HEADS).astype(f32) * k_scale
    va_c = heads(pc[2], RET_HEADS).astype(f32)
    log_g = jnp.log1p(-jnp.exp2(-decay_exp.astype(f32)))
    ya_x, ya_c = bidir_retention(qa_x, ka_x, va_x, qa_c, ka_c, va_c, log_g, need_ctx)
    qb_x = rope_2d(heads(px[4], SWA_HEADS))
    kb_x = rope_2d(heads(px[5], SWA_KV_HEADS))
    vb_x = heads(px[6], SWA_KV_HEADS)
    yb_x, yb_c = window_attention(qb_x, kb_x, vb_x, heads(pc[4], SWA_HEADS), heads(pc[5], SWA_KV_HEADS),
                                  heads(pc[6], SWA_KV_HEADS), sink, need_ctx)
    ox = jnp.concatenate([ret_out(ya_x, px[3]), merge(yb_x)], axis=-1) @ w_out
    oc = None
    if need_ctx:
        oc = jnp.concatenate([ret_out(ya_c, pc[3]), merge(yb_c)], axis=-1) @ w_out
    return ox, oc


def mixer_c(hx, hc, w_in, w_out, rpb, need_ctx):
    qx, kx, vx = (heads(a, NA_HEADS) for a in jnp.split(hx @ w_in, 3, axis=-1))
    if need_ctx:
        qc, kc, vc = (heads(a, NA_HEADS) for a in jnp.split(hc @ w_in, 3, axis=-1))
    else:
        qc = None
        kc, vc = (heads(a, NA_HEADS) for a in jnp.split(hc @ w_in[:, NA_DIM:], 2, axis=-1))
    ox, oc = neighbourhood_attention(qx, kx, vx, qc, kc, vc, rpb, need_ctx)
    return merge(ox) @ w_out, (merge(oc) @ w_out if need_ctx else None)


def conv_ffn(h, w_up, conv_w, conv_b, w_down):
    u = h @ w_up
    t = u.shape[1]
    up = jnp.pad(u, ((0, 0), (CONV_W // 2, CONV_W // 2), (0, 0)))
    u = conv_b + sum(conv_w[i] * up[:, i:i + t] for i in range(CONV_W))
    a, g = jnp.split(u, 2, axis=-1)
    return (jax.nn.silu(g) * a) @ w_down


def setup_inputs(seed: int = 0) -> dict:
    key = jax.random.key(seed)
    ks = jax.random.split(key, 20)
    f32 = jnp.float32

    def nrm(k, shape, scale):
        return jax.random.normal(k, shape, f32) * scale

    return {
        'x': nrm(ks[0], (BATCH, SEQ, D_MODEL), 1.0),
        'c': nrm(ks[1], (BATCH, D_MODEL), 1.0),
        'ctx': nrm(ks[2], (BATCH, CTX_LEN, D_MODEL), 1.0),
        'c_ctx': nrm(ks[3], (D_MODEL,), 1.0),
        'w_mod': nrm(ks[4], (DEPTH, D_MODEL, 6 * D_MODEL), 0.5 * D_MODEL ** -0.5),
        'b_mod': nrm(ks[5], (DEPTH, 6 * D_MODEL), 0.02),
        'norm_g': 1.0 + nrm(ks[6], (DEPTH, 4, D_MODEL), 0.05),
        'ab_w_in': nrm(ks[7], (N_EVEN, D_MODEL, AB_IN), D_MODEL ** -0.5),
        'ab_w_out': nrm(ks[8], (N_EVEN, AB_MIX, D_MODEL), AB_MIX ** -0.5),
        'ret_decay_exp': 5.0 + jnp.arange(RET_HEADS, dtype=f32)[None, None, :] + nrm(ks[9], (N_EVEN, 2, RET_HEADS), 0.1),
        'swa_sink': nrm(ks[10], (N_EVEN, SWA_HEADS), 1.0),
        'na_w_in': nrm(ks[11], (N_ODD, D_MODEL, 3 * NA_DIM), D_MODEL ** -0.5),
        'na_w_out': nrm(ks[12], (N_ODD, NA_DIM, D_MODEL), NA_DIM ** -0.5),
        'na_rpb': nrm(ks[13], (N_ODD, NA_HEADS, 2 * NA_ROWS - 1, 2 * NA_COLS - 1), 0.5),
        'ffn_w_up': nrm(ks[14], (DEPTH, D_MODEL, 2 * D_FF), D_MODEL ** -0.5),
        'ffn_conv_w': nrm(ks[15], (DEPTH, CONV_W, 2 * D_FF), CONV_W ** -0.5),
        'ffn_conv_b': nrm(ks[16], (DEPTH, 2 * D_FF), 0.02),
        'ffn_w_down': nrm(ks[17], (DEPTH, D_FF, D_MODEL), D_FF ** -0.5),
    }


def reference(x, c, ctx, c_ctx, w_mod, b_mod, norm_g, ab_w_in, ab_w_out, ret_decay_exp, swa_sink,
              na_w_in, na_w_out, na_rpb, ffn_w_up, ffn_conv_w, ffn_conv_b, ffn_w_down):
    b = x.shape[0]
    for l in range(DEPTH):
        last = l == DEPTH - 1
        mod_x = (jax.nn.silu(c) @ w_mod[l] + b_mod[l]).reshape(b, 6, 1, D_MODEL)
        mod_c = (jax.nn.silu(c_ctx) @ w_mod[l] + b_mod[l]).reshape(6, D_MODEL)
        hx = modulate(rmsnorm(x, norm_g[l, 0]), mod_x[:, 0], mod_x[:, 1])
        hc = modulate(rmsnorm(ctx, norm_g[l, 0]), mod_c[0], mod_c[1])
        if l % 2 == 0:
            e = l // 2
            ox, oc = mixer_ab(hx, hc, ab_w_in[e], ab_w_out[e], ret_decay_exp[e], swa_sink[e], not last)
        else:
            o = l // 2
            ox, oc = mixer_c(hx, hc, na_w_in[o], na_w_out[o], na_rpb[o], not last)
        x = x + mod_x[:, 2] * rmsnorm(ox, norm_g[l, 1])
        hx = modulate(rmsnorm(x, norm_g[l, 2]), mod_x[:, 3], mod_x[:, 4])
        x = x + mod_x[:, 5] * rmsnorm(conv_ffn(hx, ffn_w_up[l], ffn_conv_w[l], ffn_conv_b[l], ffn_w_down[l]), norm_g[l, 3])
        if not last:
            ctx = ctx + mod_c[2] * rmsnorm(oc, norm_g[l, 1])
            hc = modulate(rmsnorm(ctx, norm_g[l, 2]), mod_c[3], mod_c[4])
            ctx = ctx + mod_c[5] * rmsnorm(conv_ffn(hc, ffn_w_up[l], ffn_conv_w[l], ffn_conv_b[l], ffn_w_down[l]), norm_g[l, 3])
    return x
```

```python
import numpy as np
import concourse.bass as bass
import concourse.mybir as mybir
from concourse.bass_utils import run_bass_kernel_spmd

F32 = mybir.dt.float32
BF16 = mybir.dt.bfloat16
AF = mybir.ActivationFunctionType
ALU = mybir.AluOpType

T = 2048
L = 256
TT = T + L
NT = 18
D = 1024
KC = 8
DFF = 2816
NJ = 22
EPS = 1e-6
LN2 = 0.6931471805599453


class Buf:
    __slots__ = ("name", "w", "r")

    def __init__(self, name=""):
        self.name = name
        self.w = None
        self.r = []


class Tile:
    def __init__(self, t, name=""):
        self.t = t
        self.buf = Buf(name)

    def __getitem__(self, k):
        return self.t[k]


def _bufs(xs):
    out = []
    for x in xs:
        if isinstance(x, Buf):
            out.append(x)
        elif isinstance(x, (list, tuple)):
            out.extend(_bufs(x))
        else:
            out.append(x.buf)
    return out


EPOCH = 30000
COMPUTE = ("pe", "act", "dve", "pool")
QUEUES = ("sp", "act", "pool")


class Sched:
    def __init__(self, nc, n_dma_sems=16):
        self.nc = nc
        self.stream = {e: [] for e in ("pe", "act", "dve", "pool", "sp")}
        self.cnt = {e: 0 for e in COMPUTE}
        self.sems = {}
        self.seen = {e: {} for e in self.stream}
        self.n_dma_sems = n_dma_sems
        self.dma_rr = {q: 0 for q in QUEUES}
        self.dma_cnt = {}
        self.n_ops = 0

    def sem(self, key):
        if key not in self.sems:
            self.sems[key] = self.nc.alloc_semaphore("s_" + "_".join(str(k) for k in key))
        return self.sems[key]

    def _need(self, eng, tok, waits, raw):
        if tok is None:
            return
        key, val, teng, is_dma = tok
        if not is_dma and teng == eng:
            if eng == "pe" or not raw:
                return
        if self.seen[eng].get(key, 0) >= val:
            return
        if waits.get(key, 0) < val:
            waits[key] = val

    def _deps(self, eng, reads, writes):
        waits = {}
        for b in reads:
            self._need(eng, b.w, waits, True)
        for b in writes:
            self._need(eng, b.w, waits, False)
            for t in b.r:
                self._need(eng, t, waits, False)
        return waits

    def _commit(self, tok, reads, writes):
        for b in reads:
            b.r.append(tok)
        for b in writes:
            b.w = tok
            b.r = []

    def _cur_tok(self, e):
        c = self.cnt[e]
        if not c:
            return None
        return ((e, (c - 1) // EPOCH), (c - 1) % EPOCH + 1, e, False)

    def op(self, eng, fn, reads=(), writes=()):
        reads = _bufs(reads)
        writes = _bufs(writes)
        waits = self._deps(eng, reads, writes)
        for key, val in waits.items():
            self.stream[eng].append(("wait", key, val))
            self.seen[eng][key] = val
        self.cnt[eng] += 1
        tok = self._cur_tok(eng)
        self.sem(tok[0])
        self.stream[eng].append(("op", fn, tok[0], 1))
        self._commit(tok, reads, writes)
        self.n_ops += 1
        return tok

    def dma(self, q, fn, reads=(), writes=()):
        reads = _bufs(reads)
        writes = _bufs(writes)
        waits = self._deps(q, reads, writes)
        i = self.dma_rr[q] % self.n_dma_sems
        self.dma_rr[q] += 1
        key = ("d" + q, i)
        self.sem(key)
        prev = self.dma_cnt.get(key, 0)
        if prev and self.seen[q].get(key, 0) < prev:
            waits[key] = max(waits.get(key, 0), prev)
        for k, val in waits.items():
            self.stream[q].append(("wait", k, val))
            self.seen[q][k] = val
        val = prev + 16
        self.dma_cnt[key] = val
        self.stream[q].append(("op", fn, key, 16))
        tok = (key, val, q, True)
        self._commit(tok, reads, writes)
        self.n_ops += 1
        return tok

    def barrier(self, final=False):
        toks = [self._cur_tok(e) for e in COMPUTE]
        toks = [t for t in toks if t is not None]
        dtoks = [(k, v, k[0][1:], True) for k, v in self.dma_cnt.items() if final or k[0] != "dpool"]
        for e in self.stream:
            for tok in toks + dtoks:
                key, val, teng, is_dma = tok
                if not is_dma and teng == e:
                    continue
                if self.seen[e].get(key, 0) >= val:
                    continue
                self.stream[e].append(("wait", key, val))
                self.seen[e][key] = val

    def emit(self):
        nc = self.nc
        sems = self.sems

        def run(eng_name):
            def body(eng):
                for it in self.stream[eng_name]:
                    if it[0] == "wait":
                        eng.wait_ge(sems[it[1]], it[2])
                    else:
                        it[1](eng).then_inc(sems[it[2]], it[3])
            return body

        with nc.Block() as block:
            block.tensor(run("pe"))
            block.scalar(run("act"))
            block.vector(run("dve"))
            block.gpsimd(run("pool"))
            block.sync(run("sp"))


class Arena:
    def __init__(self, nc, base, size, tag):
        self.nc, self.base, self.size, self.tag = nc, base, size, tag
        self.off = 0
        self.n = 0

    def reset(self):
        self.off = 0

    def alloc(self, shape, dtype, name="t"):
        nbytes = int(np.prod(shape[1:])) * (2 if dtype == BF16 else 4)
        nbytes = (nbytes + 63) // 64 * 64
        assert self.off + nbytes <= self.size, (self.tag, name, self.off, nbytes, self.size)
        self.n += 1
        t = self.nc.alloc_sbuf_tensor_at(f"{self.tag}{self.n}_{name}", list(shape), dtype, offset=self.base + self.off)
        self.off += nbytes
        return Tile(t, name)


def _rope_tables():
    t = np.arange(T, dtype=np.float32)
    p = np.arange(128)
    inv = (10000.0 ** (-np.arange(64, dtype=np.float32) / 64)).astype(np.float32)
    ang = t[None, :] * inv[p % 64][:, None]
    cosA = np.cos(ang).astype(np.float32)
    sinA = np.sin(ang).astype(np.float32)
    sinA = np.where((p < 64)[:, None], -sinA, sinA).astype(np.float32)
    d = p % 64
    blk = d // 32
    idx = d % 32
    inv2 = (10000.0 ** (-np.arange(16, dtype=np.float32) / 16)).astype(np.float32)
    pos = np.where((blk == 0)[:, None], (t // 64)[None, :], (t % 64)[None, :]).astype(np.float32)
    ang2 = pos * inv2[idx % 16][:, None]
    cosB = np.cos(ang2).astype(np.float32)
    sinB = np.sin(ang2).astype(np.float32)
    sinB = np.where((idx < 16)[:, None], -sinB, sinB).astype(np.float32)
    return np.stack([cosA, sinA, cosB, sinB]).astype(np.float32)


def _ret_tables():
    j = np.arange(128, dtype=np.float32)[:, None]
    i = np.arange(128, dtype=np.float32)[None, :]
    P1 = np.maximum(i - j, 0)
    P2 = np.maximum(j - i, 0)
    Mge = (i >= j).astype(np.float32)
    Mlt = (i < j).astype(np.float32)
    IP1 = np.broadcast_to(i + 1, (128, 128))
    IB = np.broadcast_to(128 - i, (128, 128))
    tabs = np.stack([P1, P2, Mge, Mlt, IP1, IB]).astype(np.float32)
    cols = np.stack([127 - j[:, 0], j[:, 0]], axis=1).astype(np.float32)
    return np.ascontiguousarray(tabs.transpose(1, 0, 2)), cols


def _swa_masks():
    jj = np.arange(128)[:, None]
    ii = np.arange(128)[None, :]
    return np.stack([(jj >= ii), (jj <= ii)], axis=1).astype(np.float32)


def _na_geometry():
    rows = 32
    rs = lambda r: int(np.clip(r - 4, 0, rows - 8))
    qstart = np.clip(np.arange(64) - 8, 0, 48)
    colv = (np.arange(64)[None, :] >= qstart[:, None]) & (np.arange(64)[None, :] < qstart[:, None] + 16)
    pats = []
    pat_id = {}
    plan = []
    for m in range(16):
        lo = rs(2 * m) // 2
        hi = (rs(2 * m + 1) + 7) // 2
        lst = []
        for a in range(lo, hi + 1):
            mask = np.zeros((128, 128), np.float32)
            for rq in range(2):
                r = 2 * m + rq
                for rk in range(2):
                    kr = 2 * a + rk
                    if rs(r) <= kr < rs(r) + 8:
                        mask[rk * 64:(rk + 1) * 64, rq * 64:(rq + 1) * 64] = colv.T.astype(np.float32)
            if mask.sum() == 0:
                continue
            key = mask.tobytes()
            if key not in pat_id:
                pat_id[key] = len(pats)
                pats.append(mask)
            lst.append((a, a - m, pat_id[key]))
        plan.append(lst)
    return plan, np.stack(pats)


def _rpb_gather(rpb):
    jr = np.arange(128) // 64
    jc = np.arange(128) % 64
    out = np.zeros((128, 7, 8, 2, 128), np.float32)
    dc = np.clip(jc[:, None] - jc[None, :], -15, 15) + 15
    for o in range(-3, 4):
        dr = np.clip(2 * o + jr[:, None] - jr[None, :] + 7, 0, 14)
        g = rpb[:, dr, dc]
        out[:, o + 3] = g.reshape(8, 2, 128, 128).transpose(2, 0, 1, 3)
    return out


def build_program(NSEQ, na_plan, npat, dbg=None, layers=(0, 1)):
    dbg = dbg or {}
    NB = NSEQ + 1
    nc = bass.Bass("TRN2", target_bir_lowering=False)
    S = Sched(nc)

    def din(name, shape, dt=F32):
        return nc.dram_tensor(name, list(shape), dt, kind="ExternalInput").ap()

    x_in = din("x", [NSEQ, T, D])
    ctx_in = din("ctx", [NSEQ, L, D])
    cT_in = din("cT", [128, KC, NB])
    wmod_in = din("w_mod", [2, D, 6 * D])
    bmod_in = din("b_modT", [2, 128, 48])
    ng_in = din("norm_gT", [2, 128, 4, 8])
    abwin_in = din("ab_w_in", [D, 2816])
    abwout_in = din("ab_w_out", [D, D])
    decay_in = din("ret_decay", [1, 8])
    sink_in = din("swa_sink", [1, 8])
    nawin_in = din("na_w_in", [D, 3072])
    nawout_in = din("na_w_out", [D, D])
    rpbg_in = din("rpb_g", [128, 7 * 8 * 256])
    wup_in = din("ffn_w_up", [2, D, 2 * DFF])
    cw_in = din("conv_wT", [2, 128, 3, 44])
    cb_in = din("conv_bT", [2, 128, 44])
    wdown_in = din("ffn_w_down", [2, DFF, D])
    rope_in = din("rope_tab", [4, 128, T])
    rett_in = din("ret_tab", [128, 6, 128])
    retc_in = din("ret_cols", [128, 2])
    swam_in = din("swa_mask", [128, 2, 128])
    nam_in = din("na_mask", [128, npat, 128])
    out = nc.dram_tensor("out", [NSEQ, T, D], F32, kind="ExternalOutput").ap()

    def dscr(name, shape, dt):
        return nc.dram_tensor(name, list(shape), dt, kind="Internal").ap()

    ctxcur = dscr("ctxcur", [NSEQ, L, D], F32)
    ggd = dscr("ggd", [2, 2, NB, D], F32)
    NW0 = 4736
    win0s = dscr("win0s", [D, NW0], BF16)
    wout0s = dscr("wout0s", [D, D], BF16)
    win1s = dscr("win1s", [D, 3072], BF16)
    wout1s = dscr("wout1s", [D, D], BF16)
    wups = [dscr(f"wups{l}", [22, 128, KC * 256], BF16) for l in range(2)]
    wdowns = [dscr(f"wdowns{l}", [DFF, D], BF16) for l in range(2)]
    wmods = [dscr(f"wmods{l}", [D, 6 * D], BF16) for l in range(2)]
    dbg_outs = {}

    def dbg_out(name, shape):
        dbg_outs[name] = nc.dram_tensor("dbg_" + name, list(shape), F32, kind="ExternalOutput").ap()
        return dbg_outs[name]

    BASE = 16512
    o = BASE
    HTW = TT + 4

    def hcol(t):
        return t + 1 if t < T else t + 3

    hT = Tile(nc.alloc_sbuf_tensor_at("hT", [128, KC, HTW], BF16, offset=o))
    o += (KC * HTW * 2 + 63) // 64 * 64
    BIG_BASE = o
    BIG_SIZE = 53248
    YT = Tile(nc.alloc_sbuf_tensor_at("YT", [128, KC, TT], BF16, offset=BIG_BASE))
    WB = Tile(nc.alloc_sbuf_tensor_at("WB", [128, KC, 832], BF16, offset=BIG_BASE + KC * TT * 2))
    WD = Tile(nc.alloc_sbuf_tensor_at("WD", [128, NJ, D], BF16, offset=BIG_BASE))
    WUX = [Tile(nc.alloc_sbuf_tensor_at(f"WUX{k}", [128, KC, 256], BF16, offset=BIG_BASE + NJ * D * 2 + k * 4096)) for k in range(2)]
    o += BIG_SIZE
    TAB_BASE = o
    ropeT = Tile(nc.alloc_sbuf_tensor_at("ropeT", [128, 4, T], F32, offset=TAB_BASE))
    rpbT = Tile(nc.alloc_sbuf_tensor_at("rpbT", [128, 7, 8, 256], BF16, offset=TAB_BASE))
    namT = Tile(nc.alloc_sbuf_tensor_at("namT", [128, npat, 128], BF16, offset=TAB_BASE + 28672))
    assert npat <= 16
    o += 32768
    CA = Arena(nc, o, 16640, "c")
    o += 16640
    AR = Arena(nc, o, 229344 - o, "a")
    hbuf = [Buf(f"hT{i}") for i in range(NT)]
    ybuf = [Buf(f"YT{i}") for i in range(NT)]

    class PView:
        def __init__(self, base, buf=None):
            self.base = base
            self.buf = buf if buf is not None else Buf()

        def __getitem__(self, k):
            return self.base[k]

    PPt = [nc.alloc_psum_tensor(f"pp{i}", [128, 1024], F32) for i in range(4)]
    PB = [PView(PPt[i // 2][:, (i % 2) * 512:(i % 2 + 1) * 512]) for i in range(6)]
    PT = [PView(PPt[3][:, k * 512:(k + 1) * 512].bitcast(BF16)) for k in range(2)]
    PP = [PView(PPt[i][:, :]) for i in range(4)]
    rr = {"pb": 0, "pt": 0, "pp": 0}

    def pbank():
        rr["pp"] += 1
        return PP[rr["pp"] % 4]

    def bank():
        rr["pb"] += 1
        return PB[rr["pb"] % 6]

    def tbank():
        rr["pt"] += 1
        return PT[rr["pt"] % 2]

    identF = CA.alloc([128, 128], F32, "identF")
    identB = CA.alloc([128, 128], BF16, "identB")
    onesB = CA.alloc([128, 128], BF16, "onesB")
    selL = CA.alloc([128, 128], BF16, "selL")
    selR = CA.alloc([128, 128], BF16, "selR")
    swaM = CA.alloc([128, 2, 128], BF16, "swaM")
    retT = CA.alloc([128, 6, 128], F32, "retT")
    retC = CA.alloc([128, 2], F32, "retC")
    DTt = CA.alloc([128, 4, 128], F32, "DT")
    qdf = CA.alloc([128, 4, 128], F32, "qdf")
    qdb = CA.alloc([128, 4, 128], F32, "qdb")
    kdec = CA.alloc([128, 16], F32, "kdec")
    lg = CA.alloc([128, 8], F32, "lg")
    esink = CA.alloc([128, 4], F32, "esink")
    cTt = CA.alloc([128, KC, NB], F32, "cT")
    siluc = CA.alloc([128, KC, NB], BF16, "siluc")
    MOD = CA.alloc([128, 48, NB], F32, "MOD")
    bmod = CA.alloc([128, 48], F32, "bmod")
    ngT = CA.alloc([128, 4, 8], F32, "ngT")
    DER = {k: CA.alloc([128, NB, 8], F32, k) for k in ("A1", "B1", "G1", "A2", "B2", "G2")}
    cwT = CA.alloc([128, 3, 44], F32, "cwT")
    cbT = CA.alloc([128, 44], F32, "cbT")
    tmpE = CA.alloc([128, 128], F32, "tmpE")
    tmpE2 = CA.alloc([128, 128], F32, "tmpE2")
    row8 = CA.alloc([8, 128], F32, "row8")

    for pc in (0, T + 1, T + 2, HTW - 1):
        S.op("pool", lambda e, pc=pc: e.memset(hT[:, :, pc:pc + 1], 0.0), writes=[hT])
    S.op("pool", lambda e: e.memset(identF[:], 1.0), writes=[identF])
    S.op("pool", lambda e: e.affine_select(out=identF[:], in_=identF[:], pattern=[[-1, 128]], compare_op=ALU.is_equal,
                                           fill=0.0, base=0, channel_multiplier=1), reads=[identF], writes=[identF])
    S.op("dve", lambda e: e.tensor_copy(out=identB[:], in_=identF[:]), reads=[identF], writes=[identB])
    S.op("pool", lambda e: e.memset(onesB[:], 1.0), writes=[onesB])
    S.op("pool", lambda e: e.memset(selL[:], 0.0), writes=[selL])
    S.op("pool", lambda e: e.memset(selL[:, 0:64], 1.0), writes=[selL])
    S.op("pool", lambda e: e.memset(selR[:], 0.0), writes=[selR])
    S.op("pool", lambda e: e.memset(selR[:, 64:128], 1.0), writes=[selR])
    S.dma("pool", lambda e: e.dma_start(out=swaM[:], in_=swam_in), writes=[swaM])
    S.dma("sp", lambda e: e.dma_start(out=retT[:], in_=rett_in), writes=[retT])
    S.dma("sp", lambda e: e.dma_start(out=retC[:], in_=retc_in), writes=[retC])
    S.dma("sp", lambda e: e.dma_start(out=cTt[:], in_=cT_in), writes=[cTt])
    S.dma("sp", lambda e: e.dma_start(out=lg[:], in_=decay_in.partition_broadcast(128)), writes=[lg])
    for col in range(4):
        for half in range(2):
            hh = (col // 2) * 4 + (col % 2) * 2 + half
            S.dma("sp", lambda e, col=col, half=half, hh=hh: e.dma_start(
                out=esink[half * 64:(half + 1) * 64, col:col + 1],
                in_=sink_in[0:1, hh:hh + 1].partition_broadcast(64)), writes=[esink])
    S.op("act", lambda e: e.activation(out=esink[:], in_=esink[:], func=AF.Exp), reads=[esink], writes=[esink])
    S.op("act", lambda e: e.activation(out=siluc[:], in_=cTt[:], func=AF.Silu), reads=[cTt], writes=[siluc])
    S.op("act", lambda e: e.activation(out=lg[:], in_=lg[:], func=AF.Exp, scale=-LN2), reads=[lg], writes=[lg])
    S.op("act", lambda e: e.activation(out=lg[:], in_=lg[:], func=AF.Ln, scale=-1.0, bias=1.0), reads=[lg], writes=[lg])
    for h in range(4):
        lf = lg[:, h:h + 1]
        lb = lg[:, 4 + h:5 + h]
        S.op("act", lambda e, lf=lf: e.activation(out=tmpE[:], in_=retT[:, 0, :], func=AF.Exp, scale=lf), reads=[retT, lg], writes=[tmpE])
        S.op("act", lambda e, lb=lb: e.activation(out=tmpE2[:], in_=retT[:, 1, :], func=AF.Exp, scale=lb), reads=[retT, lg], writes=[tmpE2])
        S.op("dve", lambda e: e.tensor_tensor(out=tmpE[:], in0=tmpE[:], in1=retT[:, 2, :], op=ALU.mult), reads=[tmpE, retT], writes=[tmpE])
        S.op("dve", lambda e: e.tensor_tensor(out=tmpE2[:], in0=tmpE2[:], in1=retT[:, 3, :], op=ALU.mult), reads=[tmpE2, retT], writes=[tmpE2])
        S.op("dve", lambda e, h=h: e.tensor_tensor(out=DTt[:, h, :], in0=tmpE[:], in1=tmpE2[:], op=ALU.add), reads=[tmpE, tmpE2], writes=[DTt])
        S.op("act", lambda e, h=h, lf=lf: e.activation(out=qdf[:, h, :], in_=retT[:, 4, :], func=AF.Exp, scale=lf), reads=[retT, lg], writes=[qdf])
        S.op("act", lambda e, h=h, lb=lb: e.activation(out=qdb[:, h, :], in_=retT[:, 5, :], func=AF.Exp, scale=lb), reads=[retT, lg], writes=[qdb])
        S.op("act", lambda e, h=h, lf=lf: e.activation(out=kdec[:, h:h + 1], in_=retC[:, 0:1], func=AF.Exp, scale=lf), reads=[retC, lg], writes=[kdec])
        S.op("act", lambda e, h=h, lb=lb: e.activation(out=kdec[:, 4 + h:5 + h], in_=retC[:, 1:2], func=AF.Exp, scale=lb), reads=[retC, lg], writes=[kdec])
        S.op("act", lambda e, h=h: e.activation(out=kdec[:, 8 + h:9 + h], in_=lg[:, h:h + 1], func=AF.Exp, scale=128.0), reads=[lg], writes=[kdec])
        S.op("act", lambda e, h=h: e.activation(out=kdec[:, 12 + h:13 + h], in_=lg[:, 4 + h:5 + h], func=AF.Exp, scale=128.0), reads=[lg], writes=[kdec])

    class Staged:
        def __init__(self, ap):
            self.ap = ap
            self.bufs = []

    def stage_cols(st, dst_c, src, src_c, n, rb):
        rows = src.shape[0]
        for r0 in range(0, rows, rb):
            r1 = min(rows, r0 + rb)
            b = Buf()
            st.bufs.append(b)
            S.dma("pool", lambda e, r0=r0, r1=r1: e.dma_start(out=st.ap[r0:r1, dst_c:dst_c + n], in_=src[r0:r1, src_c:src_c + n]), writes=[b])

    def stage_runs(st, dst_c, src, runs):
        for (sc, n) in runs:
            stage_cols(st, dst_c, src, sc, n, dbg.get("rbn", 1024) if n < 128 else dbg.get("rbw", 1024))
            dst_c += n
        return dst_c

    s_win0, s_wout0, s_win1, s_wout1 = Staged(win0s), Staged(wout0s), Staged(win1s), Staged(wout1s)
    s_wup = [Staged(a) for a in wups]
    s_wdown = [Staged(a) for a in wdowns]
    s_wmod = [Staged(a) for a in wmods]
    def stage_layer(l, part):
        if part == 0:
            stage_cols(s_wmod[l], 0, wmod_in[l], 0, 6 * D, dbg.get("rbf", 512))
        elif part == 1 and l == 0:
            c = 0
            for h in range(4):
                qa, ka, va, ga = h * 128, 512 + h * 128, 1024 + h * 128, 1536 + h * 128
                c = stage_runs(s_win0, c, abwin_in, [(qa, 128), (qa + 64, 64), (qa, 64), (ka, 128), (ka + 64, 64), (ka, 64), (ga, 128), (va, 128)])
            for g in range(2):
                def sw64(b0):
                    return [(b0 + 16, 16), (b0, 16), (b0 + 48, 16), (b0 + 32, 16)]
                kb = 2560 + g * 64
                vb = 2688 + g * 64
                runs = []
                for qc in range(2):
                    q0 = 2048 + (4 * g + 2 * qc) * 64
                    runs += [(q0, 128)] + sw64(q0) + sw64(q0 + 64)
                runs += [(kb, 64), (kb, 64)] + sw64(kb) + sw64(kb) + [(vb, 64)]
                c = stage_runs(s_win0, c, abwin_in, runs)
            assert c == NW0
            stage_cols(s_wout0, 0, abwout_in, 0, D, dbg.get("rbf", 512))
        elif part == 1 and l == 1:
            c = 0
            for hc in range(8):
                c = stage_runs(s_win1, c, nawin_in, [(hc * 128, 128), (1024 + hc * 128, 128), (2048 + hc * 128, 128)])
            stage_cols(s_wout1, 0, nawout_in, 0, D, dbg.get("rbf", 512))
        elif part == 2:
            for blk in range(22):
                c0 = (blk * 256) if blk < 11 else (DFF + (blk - 11) * 256)
                b_ = Buf()
                s_wup[l].bufs.append(b_)
                S.dma("pool", lambda e, l=l, blk=blk, c0=c0: e.dma_start(out=wups[l][blk].rearrange("p (kc n) -> p kc n", kc=KC),
                                                                         in_=wup_in[l].rearrange("(kc p) n -> p kc n", p=128)[:, :, c0:c0 + 256]), writes=[b_])
        elif part == 3:
            stage_cols(s_wdown[l], 0, wdown_in[l], 0, D, dbg.get("rbf", 512))

    first_layer = layers[0]
    for part in range(4):
        stage_layer(first_layer, part)
    deferred = [(l, part) for l in layers[1:] for part in range(4)]

    def load_w(q, dst, dst_ap, st, c0, n):
        src = st.ap.rearrange("(kc p) n -> p kc n", p=128)[:, :, c0:c0 + n]
        S.dma(q, lambda e: e.dma_start(out=dst_ap, in_=src), reads=st.bufs, writes=[dst])

    xbuf = {}

    def xb(s, i):
        if (s, i) not in xbuf:
            xbuf[(s, i)] = Buf(f"x{s}_{i}")
        return xbuf[(s, i)]

    def x_ap(l, s, i, first):
        if i < 16:
            src = x_in if (first and l == 0) else out
            return src[s, i * 128:(i + 1) * 128, :]
        src = ctx_in if (first and l == 0) else ctxcur
        return src[s, (i - 16) * 128:(i - 15) * 128, :]

    def x_dst(s, i):
        if i < 16:
            return out[s, i * 128:(i + 1) * 128, :]
        return ctxcur[s, (i - 16) * 128:(i - 15) * 128, :]

    def mod_phase(l):
        S.barrier()
        AR.reset()
        S.dma("sp", lambda e: e.dma_start(out=bmod[:], in_=bmod_in[l]), writes=[bmod])
        S.dma("sp", lambda e: e.dma_start(out=ngT[:], in_=ng_in[l]), writes=[ngT])
        S.dma("sp", lambda e: e.dma_start(out=cwT[:], in_=cw_in[l]), writes=[cwT])
        S.dma("sp", lambda e: e.dma_start(out=cbT[:], in_=cb_in[l]), writes=[cbT])
        Wm = [AR.alloc([128, KC, 512], BF16, f"wm{i}") for i in range(2)]
        psM = bank()
        for blk in range(12):
            w = Wm[blk % 2]
            load_w("sp", w, w[:], s_wmod[l], blk * 512, 512)
            for cc in range(4):
                ch = blk * 4 + cc
                for kc in range(KC):
                    S.op("pe", lambda e, w=w, cc=cc, ch=ch, kc=kc: e.matmul(
                        psM[:, ch * NB:(ch + 1) * NB], lhsT=w[:, kc, cc * 128:(cc + 1) * 128], rhs=siluc[:, kc, :],
                        start=(kc == 0), stop=(kc == KC - 1)), reads=[w, siluc], writes=[psM])
        S.op("dve", lambda e: e.tensor_tensor(out=MOD[:], in0=psM[:, 0:48 * NB].rearrange("p (c b) -> p c b", b=NB),
                                              in1=bmod[:].unsqueeze(2).to_broadcast([128, 48, NB]), op=ALU.add),
             reads=[psM, bmod], writes=[MOD])

        def mv(g):
            return MOD[:, g * 8:(g + 1) * 8, :].rearrange("p c b -> p b c")

        def gb(k):
            return ngT[:, k, :].unsqueeze(1).to_broadcast([128, NB, 8])

        for (nm, gs, gk, plus1) in (("A1", 1, 0, True), ("G1", 2, 1, False), ("A2", 4, 2, True), ("G2", 5, 3, False)):
            d = DER[nm]
            if plus1:
                S.op("dve", lambda e, d=d, gs=gs, gk=gk: e.scalar_tensor_tensor(
                    out=d[:], in0=mv(gs), scalar=1.0, in1=gb(gk), op0=ALU.add, op1=ALU.mult), reads=[MOD, ngT], writes=[d])
            else:
                S.op("dve", lambda e, d=d, gs=gs, gk=gk: e.tensor_tensor(out=d[:], in0=mv(gs), in1=gb(gk), op=ALU.mult),
                     reads=[MOD, ngT], writes=[d])
        S.op("dve", lambda e: e.tensor_copy(out=DER["B1"][:], in_=mv(0)), reads=[MOD], writes=[DER["B1"]])
        S.op("dve", lambda e: e.tensor_copy(out=DER["B2"][:], in_=mv(3)), reads=[MOD], writes=[DER["B2"]])
        for wi, nm in enumerate(("G1", "G2")):
            for b in range(NB):
                ps = bank()
                S.op("pe", lambda e, ps=ps, nm=nm, b=b: e.transpose(out=ps[0:8, 0:128], in_=DER[nm][:, b, :], identity=identF[:]),
                     reads=[DER[nm], identF], writes=[ps])
                S.op("dve", lambda e, ps=ps: e.tensor_copy(out=row8[:], in_=ps[0:8, 0:128]), reads=[ps], writes=[row8])
                gbuf = ggbuf[(l, wi, b)] = Buf()
                S.dma("sp", lambda e, wi=wi, b=b: e.dma_start(out=ggd[l, wi, b].rearrange("(a c) -> a c", c=128), in_=row8[:]),
                      reads=[row8], writes=[gbuf])

    ggbuf = {}

    def norm_a(X, i, tl):
        ss, junk, xs = tl["ss"], tl["junk"], tl["xs"]
        S.op("act", lambda e: e.activation(out=junk[:], in_=X[:], func=AF.Square, accum_out=ss[:, 0:1]), reads=[X], writes=[junk, ss])
        S.op("act", lambda e: e.activation(out=ss[:, 1:2], in_=ss[:, 0:1], func=AF.Sqrt, bias=EPS, scale=1.0 / D), reads=[ss], writes=[ss])
        S.op("dve", lambda e: e.reciprocal(out=ss[:, 2:3], in_=ss[:, 1:2]), reads=[ss], writes=[ss])
        S.op("dve", lambda e: e.tensor_scalar(out=xs[:], in0=X[:], scalar1=ss[:, 2:3], scalar2=None, op0=ALU.mult), reads=[X, ss], writes=[xs])
        pt = tbank()
        for c in range(KC):
            S.op("pe", lambda e, c=c: e.transpose(out=pt[:, c * 128:(c + 1) * 128], in_=xs[:, c * 128:(c + 1) * 128], identity=identB[:]),
                 reads=[xs, identB], writes=[pt])
        return pt

    def norm_b(pt, i, Acol, Bcol, tl):
        tmp = tl["tmp"]
        S.op("dve", lambda e: e.tensor_tensor(out=tmp[:].rearrange("p (c t) -> p c t", t=128), in0=pt[:].rearrange("p (c t) -> p c t", t=128),
                                              in1=Acol.unsqueeze(2).to_broadcast([128, KC, 128]), op=ALU.mult), reads=[pt, DER["A1"], DER["A2"]], writes=[tmp])
        S.op("pool", lambda e: e.tensor_tensor(out=hT[:, :, hcol(i * 128):hcol(i * 128) + 128], in0=tmp[:].rearrange("p (c t) -> p c t", t=128),
                                               in1=Bcol.unsqueeze(2).to_broadcast([128, KC, 128]), op=ALU.add), reads=[tmp, DER["B1"], DER["B2"], hT], writes=[hbuf[i]])

    def norm_scratch(k):
        return dict(ss=AR.alloc([128, 4], F32, f"ss{k}"), junk=AR.alloc([128, D], BF16, f"junk{k}"),
                    xs=AR.alloc([128, D], BF16, f"xs{k}"), tmp=AR.alloc([128, D], F32, f"tmp{k}"))

    def n1_phase(l, s, first):
        S.barrier()
        AR.reset()
        Xs = [AR.alloc([128, D], F32, f"X{k}") for k in range(2)]
        sc = [norm_scratch(k) for k in range(2)]
        pend = []
        for i in range(NT):
            X = Xs[i % 2]
            S.dma("sp", lambda e, X=X, i=i: e.dma_start(out=X[:], in_=x_ap(l, s, i, first)), reads=[xb(s, i)], writes=[X])
            b = s if i < 16 else NB - 1
            pt = norm_a(X, i, sc[i % 2])
            pend.append((pt, i, DER["A1"][:, b, :], DER["B1"][:, b, :], sc[i % 2]))
            if len(pend) > 1:
                norm_b(*pend.pop(0))
        while pend:
            norm_b(*pend.pop(0))

    def projT(ps, w, wc0, t0, n):
        tiles = [hbuf[i] for i in range(t0 // 128, (t0 + n + 127) // 128)]
        for kc in range(KC):
            S.op("pe", lambda e, kc=kc: e.matmul(ps[:, 0:n], lhsT=w[:, kc, wc0:wc0 + 128], rhs=hT[:, kc, hcol(t0):hcol(t0) + n],
                                                 start=(kc == 0), stop=(kc == KC - 1)), reads=[w] + tiles, writes=[ps])

    def projTok(ps_ap, ps, w, wc0, n, i):
        for kc in range(KC):
            S.op("pe", lambda e, kc=kc: e.matmul(ps_ap, lhsT=hT[:, kc, hcol(i * 128):hcol(i * 128) + 128], rhs=w[:, kc, wc0:wc0 + n],
                                                 start=(kc == 0), stop=(kc == KC - 1)), reads=[w, hbuf[i]], writes=[ps])

    SUP = [(0, 512), (512, 512), (1024, 512), (1536, 512), (2048, 256)]

    def rope_proj(w, c_x, c_sw, dst, tabc, tabs, t1, t2):
        for (t0, n) in SUP:
            pq = bank()
            projT(pq, w, c_x, t0, n)
            if t0 < T:
                pqs = bank()
                projT(pqs, w, c_sw, t0, n)
                S.op("dve", lambda e, pq=pq, t0=t0, n=n: e.tensor_tensor(out=t1[:, 0:n], in0=pq[:, 0:n], in1=ropeT[:, tabc, t0:t0 + n], op=ALU.mult),
                     reads=[pq, ropeT], writes=[t1])
                S.op("dve", lambda e, pqs=pqs, t0=t0, n=n: e.tensor_tensor(out=t2[:, 0:n], in0=pqs[:, 0:n], in1=ropeT[:, tabs, t0:t0 + n], op=ALU.mult),
                     reads=[pqs, ropeT], writes=[t2])
                S.op(dbg.get("rope_add", "dve"), lambda e, t0=t0, n=n: e.tensor_tensor(out=dst[:, t0:t0 + n], in0=t1[:, 0:n], in1=t2[:, 0:n], op=ALU.add),
                     reads=[t1, t2], writes=[dst])
            else:
                S.op("act", lambda e, pq=pq, t0=t0, n=n: e.copy(out=dst[:, t0:t0 + n], in_=pq[:, 0:n]), reads=[pq], writes=[dst])

    def mix0_phase(s):
        S.barrier()
        AR.reset()
        t1 = AR.alloc([128, 512], F32, "t1")
        t2 = AR.alloc([128, 512], F32, "t2")
        t3 = t2
        KVf = AR.alloc([128, NT, 128], BF16, "KVf")
        KVb = AR.alloc([128, NT, 128], BF16, "KVb")
        ysq2 = [AR.alloc([128, 512], BF16, f"ysq{k}") for k in range(2)]
        slot = lambda nm, shape=(128, TT): AR.alloc(list(shape), BF16, nm)
        qT, kT, gsT = slot("qT"), slot("kT"), slot("gs")
        Vt = slot("V", (128, NT, 128))
        kf, kb_ = slot("kf", (128, NT, 128)), slot("kb", (128, NT, 128))
        Sfb, Sbb = slot("Sfb", (128, NT, 128)), slot("Sbb", (128, NT, 128))
        inner = [AR.alloc([128, 4, 128], BF16, f"in{k}") for k in range(2)]
        qfT = [AR.alloc([128, 4, 128], BF16, f"qf{k}") for k in range(2)]
        qbT = [AR.alloc([128, 4, 128], BF16, f"qb{k}") for k in range(2)]
        Sst = [AR.alloc([128, 128], F32, f"S{k}") for k in range(4)]
        if ropeT.buf.w is None or tabstate["cur"] != 0:
            S.dma("sp", lambda e: e.dma_start(out=ropeT[:], in_=rope_in.rearrange("a p t -> p a t")), writes=[ropeT])
            tabstate["cur"] = 0
        WB2 = AR.alloc([128, KC, 768], BF16, "WB2")
        WBs = [WB, WB2]
        load_w("sp", WBs[0], WBs[0][:, :, 0:768], s_win0, 0, 768)
        for h in range(4):
            WBh = WBs[h % 2]
            if h + 1 < 4:
                load_w("sp", WBs[(h + 1) % 2], WBs[(h + 1) % 2][:, :, 0:768], s_win0, (h + 1) * 768, 768)
            rope_proj(WBh, 0, 128, qT, 0, 1, t1, t2)
            rope_proj(WBh, 256, 384, kT, 0, 1, t1, t2)
            for i0 in range(0, NT, 4):
                ni = min(4, NT - i0)
                pv = bank()
                for k in range(ni):
                    projTok(pv[:, k * 128:(k + 1) * 128], pv, WBh, 640, 128, i0 + k)
                S.op("act", lambda e, pv=pv, i0=i0, ni=ni: e.copy(out=Vt[:, i0:i0 + ni, :], in_=pv[:, 0:ni * 128].rearrange("p (a b) -> p a b", b=128)),
                     reads=[pv], writes=[Vt])
            if dbg.get("m0", 99) <= 1:
                return
            for i0 in range(0, NT, 8):
                ni = min(8, NT - i0)
                pt = tbank()
                for k in range(ni):
                    n_ = i0 + k
                    S.op("pe", lambda e, k=k, n_=n_, pt=pt: e.transpose(out=pt[:, k * 128:(k + 1) * 128], in_=kT[:, n_ * 128:(n_ + 1) * 128], identity=identB[:]),
                         reads=[kT, identB], writes=[pt])
                if dbg.get("kfmode", 0) == 1:
                    continue
                if dbg.get("kfmode", 0) == 2:
                    S.op("dve", lambda e, pt=pt, i0=i0, ni=ni, h=h: e.tensor_scalar(out=kf[:, i0:i0 + ni, :], in0=pt[:, 0:ni * 128].rearrange("p (a b) -> p a b", b=128),
                                                                                    scalar1=kdec[:, h:h + 1], scalar2=None, op0=ALU.mult), reads=[pt, kdec], writes=[kf])
                    S.op("dve", lambda e, pt=pt, i0=i0, ni=ni, h=h: e.tensor_scalar(out=kb_[:, i0:i0 + ni, :], in0=pt[:, 0:ni * 128].rearrange("p (a b) -> p a b", b=128),
                                                                                    scalar1=kdec[:, 4 + h:5 + h], scalar2=None, op0=ALU.mult), reads=[pt, kdec], writes=[kb_])
                    continue
                S.op("act", lambda e, pt=pt, i0=i0, ni=ni, h=h: e.activation(out=kf[:, i0:i0 + ni, :], in_=pt[:, 0:ni * 128].rearrange("p (a b) -> p a b", b=128),
                                                                       func=AF.Identity, scale=kdec[:, h:h + 1]), reads=[pt, kdec], writes=[kf])
                S.op("act", lambda e, pt=pt, i0=i0, ni=ni, h=h: e.activation(out=kb_[:, i0:i0 + ni, :], in_=pt[:, 0:ni * 128].rearrange("p (a b) -> p a b", b=128),
                                                                       func=AF.Identity, scale=kdec[:, 4 + h:5 + h]), reads=[pt, kdec], writes=[kb_])
            if dbg.get("m0", 99) <= 2:
                return
            has = {}
            chains = {"f": ([16, 17] + list(range(16)), kf, 8 + h, Sfb, Sst[0:2]),
                      "b": ([17, 16] + list(range(15, -1, -1)), kb_, 12 + h, Sbb, Sst[2:4])}
            KVd = {"f": KVf, "b": KVb}
            for tag in ("f", "b"):
                order, kd, cd, Sb_, Sp = chains[tag]
                KVt = KVd[tag]
                for g0 in range(0, NT, 4):
                    ng_ = min(4, NT - g0)
                    pk = bank()
                    for k in range(ng_):
                        n_ = g0 + k
                        S.op("pe", lambda e, pk=pk, kd=kd, n_=n_, k=k: e.matmul(pk[:, k * 128:(k + 1) * 128], lhsT=kd[:, n_, :], rhs=Vt[:, n_, :], start=True, stop=True),
                             reads=[kd, Vt], writes=[pk])
                    eng = "act" if tag == "f" else "dve"
                    if eng == "act":
                        S.op("act", lambda e, pk=pk, g0=g0, ng_=ng_, KVt=KVt: e.copy(out=KVt[:, g0:g0 + ng_, :], in_=pk[:, 0:ng_ * 128].rearrange("p (a b) -> p a b", b=128)),
                             reads=[pk], writes=[KVt])
                    else:
                        S.op("dve", lambda e, pk=pk, g0=g0, ng_=ng_, KVt=KVt: e.tensor_copy(out=KVt[:, g0:g0 + ng_, :], in_=pk[:, 0:ng_ * 128].rearrange("p (a b) -> p a b", b=128)),
                             reads=[pk], writes=[KVt])
            for (t0, n) in SUP:
                pg = bank()
                projT(pg, WBh, 512, t0, n)
                S.op("act", lambda e, pg=pg, t0=t0, n=n: e.activation(out=gsT[:, t0:t0 + n], in_=pg[:, 0:n], func=AF.Silu), reads=[pg], writes=[gsT])
            for idx in range(NT):
                for tag in ("f", "b"):
                    order, kd, cd, Sb_, Sp = chains[tag]
                    KVt = KVd[tag]
                    n_ = order[idx]
                    cur, nxt = Sp[idx % 2], Sp[(idx + 1) % 2]
                    has[(tag, n_)] = idx > 0
                    if idx > 0:
                        S.op("act", lambda e, cur=cur, Sb_=Sb_, n_=n_: e.copy(out=Sb_[:, n_, :], in_=cur[:]), reads=[cur], writes=[Sb_])
                    if idx < NT - 1:
                        if idx == 0:
                            S.op("dve", lambda e, nxt=nxt, n_=n_, KVt=KVt: e.tensor_copy(out=nxt[:], in_=KVt[:, n_, :]), reads=[KVt], writes=[nxt])
                        else:
                            S.op("dve", lambda e, cur=cur, nxt=nxt, cd=cd, n_=n_, KVt=KVt: e.scalar_tensor_tensor(out=nxt[:], in0=cur[:], scalar=kdec[:, cd:cd + 1], in1=KVt[:, n_, :],
                                                                                                          op0=ALU.mult, op1=ALU.add), reads=[KVt, cur, kdec], writes=[nxt])
            if dbg.get("m0", 99) <= 3:
                return
            tails = []
            for gi, (t0, n) in enumerate(SUP):
                ng = n // 128
                c0 = t0 // 128
                pi = bank()
                for k in range(ng):
                    cs = slice((c0 + k) * 128, (c0 + k + 1) * 128)
                    S.op("pe", lambda e, pi=pi, k=k, cs=cs: e.matmul(pi[:, k * 128:(k + 1) * 128], lhsT=kT[:, cs], rhs=qT[:, cs], start=True, stop=True),
                         reads=[kT, qT], writes=[pi])
                inn, qf_, qb_ = inner[gi % 2], qfT[gi % 2], qbT[gi % 2]
                S.op("dve", lambda e, pi=pi, inn=inn, ng=ng, h=h: e.tensor_tensor(out=inn[:, 0:ng, :], in0=pi[:, 0:ng * 128].rearrange("p (a b) -> p a b", b=128),
                                                                             in1=DTt[:, h, :].unsqueeze(1).to_broadcast([128, ng, 128]), op=ALU.mult),
                     reads=[pi, DTt], writes=[inn])
                S.op("pool", lambda e, qf_=qf_, ng=ng, t0=t0, n=n, h=h: e.tensor_tensor(out=qf_[:, 0:ng, :], in0=qT[:, t0:t0 + n].rearrange("p (a b) -> p a b", b=128),
                                                                                   in1=qdf[:, h, :].unsqueeze(1).to_broadcast([128, ng, 128]), op=ALU.mult),
                     reads=[qT, qdf], writes=[qf_])
                S.op("pool", lambda e, qb_=qb_, ng=ng, t0=t0, n=n, h=h: e.tensor_tensor(out=qb_[:, 0:ng, :], in0=qT[:, t0:t0 + n].rearrange("p (a b) -> p a b", b=128),
                                                                                   in1=qdb[:, h, :].unsqueeze(1).to_broadcast([128, ng, 128]), op=ALU.mult),
                     reads=[qT, qdb], writes=[qb_])
                po = bank()
                for k in range(ng):
                    n_ = c0 + k
                    terms = [(Vt[:, n_, :], inn[:, k, :], [Vt, inn])]
                    if has[("f", n_)]:
                        terms.append((Sfb[:, n_, :], qf_[:, k, :], [Sfb, qf_]))
                    if has[("b", n_)]:
                        terms.append((Sbb[:, n_, :], qb_[:, k, :], [Sbb, qb_]))
                    for ti, (lh, rh, rd) in enumerate(terms):
                        S.op("pe", lambda e, po=po, k=k, lh=lh, rh=rh, ti=ti, nt=len(terms): e.matmul(
                            po[:, k * 128:(k + 1) * 128], lhsT=lh, rhs=rh, start=(ti == 0), stop=(ti == nt - 1)), reads=rd, writes=[po])
                ysq_ = ysq2[gi % 2]
                S.op("act", lambda e, po=po, n=n, ysq_=ysq_: e.activation(out=ysq_[:, 0:n], in_=po[:, 0:n], func=AF.Square), reads=[po], writes=[ysq_])

                def ret_tail(po=po, n=n, t0=t0, ysq_=ysq_, h=h):
                    pn = bank()
                    S.op("pe", lambda e: e.matmul(pn[:, 0:n], lhsT=onesB[:], rhs=ysq_[:, 0:n], start=True, stop=True), reads=[onesB, ysq_], writes=[pn])
                    S.op("act", lambda e: e.activation(out=t3[:, 0:n], in_=pn[:, 0:n], func=AF.Sqrt, bias=EPS * 128.0, scale=1.0 / 128), reads=[pn], writes=[t3])
                    S.op("dve", lambda e: e.reciprocal(out=t3[:, 0:n], in_=t3[:, 0:n]), reads=[t3], writes=[t3])
                    S.op("dve", lambda e: e.tensor_tensor(out=t1[:, 0:n], in0=po[:, 0:n], in1=t3[:, 0:n], op=ALU.mult), reads=[po, t3], writes=[t1])
                    S.op("pool", lambda e: e.tensor_tensor(out=YT[:, h, t0:t0 + n], in0=t1[:, 0:n], in1=gsT[:, t0:t0 + n], op=ALU.mult),
                         reads=[t1, gsT], writes=[ybuf[i] for i in range(t0 // 128, (t0 + n) // 128)])

                tails.append(ret_tail)
                if len(tails) > 1:
                    tails.pop(0)()
            while tails:
                tails.pop(0)()
        if dbg.get("m0", 99) <= 4:
            return
        S.barrier()
        AR.reset()
        u1 = AR.alloc([128, 512], F32, "u1")
        u2 = AR.alloc([128, 512], F32, "u2")
        Q = [AR.alloc([128, TT], BF16, f"Q{k}") for k in range(2)]
        Kd = AR.alloc([128, TT], BF16, "Kd")
        KdM = [AR.alloc([128, TT], BF16, f"KdM{k}") for k in range(2)]
        S.op("pool", lambda e: e.memset(KdM[0][64:128, :], 0.0), writes=[KdM[0]])
        S.op("pool", lambda e: e.memset(KdM[1][0:64, :], 0.0), writes=[KdM[1]])
        Vz = AR.alloc([128, NT, 2, 128], BF16, "Vz")
        PTt = [AR.alloc([128, 5, 2, 128], BF16, f"PT{k}") for k in range(3)]
        PTb = [[Buf() for _ in range(5)] for k in range(3)]
        dn = [AR.alloc([128, 128], F32, f"dn{k}") for k in range(3)]
        WB3 = AR.alloc([128, KC, 832], BF16, "WB3")
        WBg = [WB, WB3]
        load_w("sp", WBg[0], WBg[0][:, :, 0:832], s_win0, 3072, 832)
        load_w("sp", WBg[1], WBg[1][:, :, 0:832], s_win0, 3072 + 832, 832)
        for g in range(2):
            WBc = WBg[g]
            rope_proj(WBc, 0, 128, Q[0], 2, 3, u1, u2)
            rope_proj(WBc, 256, 384, Q[1], 2, 3, u1, u2)
            rope_proj(WBc, 512, 640, Kd, 2, 3, u1, u2)
            S.op("act", lambda e: e.copy(out=KdM[0][0:64, :], in_=Kd[0:64, :]), reads=[Kd], writes=[KdM[0]])
            S.op("dve", lambda e: e.tensor_copy(out=KdM[1][64:128, :], in_=Kd[64:128, :]), reads=[Kd], writes=[KdM[1]])
            S.op("pool", lambda e: e.memset(Vz[:], 0.0), writes=[Vz])
            for i0 in range(0, NT, 8):
                ni = min(8, NT - i0)
                pv = bank()
                for k in range(ni):
                    projTok(pv[:, k * 64:(k + 1) * 64], pv, WBc, 768, 64, i0 + k)
                S.op("act", lambda e, pv=pv, i0=i0, ni=ni: e.copy(out=Vz[:, i0:i0 + ni, 0, 0:64], in_=pv[:, 0:ni * 64].rearrange("p (a b) -> p a b", b=64)),
                     reads=[pv], writes=[Vz])
                S.op("dve", lambda e, pv=pv, i0=i0, ni=ni: e.tensor_copy(out=Vz[:, i0:i0 + ni, 1, 64:128], in_=pv[:, 0:ni * 64].rearrange("p (a b) -> p a b", b=64)),
                     reads=[pv], writes=[Vz])
            cnt = 0
            if dbg.get("swa", 99) <= 1:
                return
            def swa_a(it, qc, Pt, Pb):
                if it < 16:
                    kts = [(kt, (0 if kt < it else (1 if kt > it else None))) for kt in (it - 1, it, it + 1) if 0 <= kt < 16] + [(16, None), (17, None)]
                else:
                    kts = [(16, None), (17, None)]
                qs = slice(it * 128, (it + 1) * 128)
                for b0 in range(0, len(kts), 2):
                    nb = min(2, len(kts) - b0)
                    ps = bank()
                    for k in range(nb):
                        kt = kts[b0 + k][0]
                        ks = slice(kt * 128, (kt + 1) * 128)
                        for hh in range(2):
                            S.op("pe", lambda e, ps=ps, k=k, hh=hh, ks=ks, qs=qs, qc=qc: e.matmul(
                                ps[:, (k * 2 + hh) * 128:(k * 2 + hh + 1) * 128], lhsT=KdM[hh][:, ks], rhs=Q[qc][:, qs], start=True, stop=True),
                                reads=[KdM[hh], Q[qc]], writes=[ps])
                    S.op("act", lambda e, ps=ps, Pt=Pt, b0=b0, nb=nb: e.activation(
                        out=Pt[:, b0:b0 + nb, :, :].rearrange("p a h b -> p (a h b)"), in_=ps[:, 0:nb * 256], func=AF.Exp, scale=0.125),
                        reads=[ps], writes=[Pb[b0 + k] for k in range(nb)])
                for k, (kt, mk) in enumerate(kts):
                    if mk is not None:
                        S.op("pool", lambda e, Pt=Pt, k=k, mk=mk: e.tensor_tensor(out=Pt[:, k, :, :], in0=Pt[:, k, :, :],
                                                                                in1=swaM[:, mk, :].unsqueeze(1).to_broadcast([128, 2, 128]), op=ALU.mult),
                             reads=[Pb[k], swaM], writes=[Pb[k]])
                return kts

            def swa_b(it, qc, Pt, Pb, dnt, kts):
                qs = slice(it * 128, (it + 1) * 128)
                po = bank()
                nk = len(kts)
                for k, (kt, mk) in enumerate(kts):
                    for hh in range(2):
                        S.op("pe", lambda e, po=po, Pt=Pt, k=k, kt=kt, hh=hh, nk=nk: e.matmul(
                            po[:, 0:128], lhsT=Vz[:, kt, hh, :], rhs=Pt[:, k, hh, :], start=(k == 0 and hh == 0), stop=(k == nk - 1 and hh == 1)),
                            reads=[Vz, Pb[k]], writes=[po])
                for k, (kt, mk) in enumerate(kts):
                    for hh in range(2):
                        sel = selL if hh == 0 else selR
                        S.op("pe", lambda e, po=po, Pt=Pt, k=k, hh=hh, nk=nk, sel=sel: e.matmul(
                            po[:, 128:256], lhsT=sel[:], rhs=Pt[:, k, hh, :], start=(k == 0 and hh == 0), stop=(k == nk - 1 and hh == 1)),
                            reads=[sel, Pb[k]], writes=[po])
                col = 2 * g + qc
                S.op("dve", lambda e, po=po, dnt=dnt, col=col: e.tensor_scalar(out=dnt[:], in0=po[:, 128:256], scalar1=esink[:, col:col + 1], scalar2=None, op0=ALU.add),
                     reads=[po, esink], writes=[dnt])
                S.op("dve", lambda e, dnt=dnt: e.reciprocal(out=dnt[:], in_=dnt[:]), reads=[dnt], writes=[dnt])
                S.op("dve", lambda e, po=po, dnt=dnt, col=col, qs=qs: e.tensor_tensor(out=YT[:, 4 + col, qs], in0=po[:, 0:128], in1=dnt[:], op=ALU.mult),
                     reads=[po, dnt], writes=[ybuf[it]])

            pend = []
            for it in range(NT):
                for qc in range(2):
                    Pt, Pb, dnt = PTt[cnt % 3], PTb[cnt % 3], dn[cnt % 3]
                    cnt += 1
                    kts = swa_a(it, qc, Pt, Pb)
                    pend.append((it, qc, Pt, Pb, dnt, kts))
                    if len(pend) > 2:
                        swa_b(*pend.pop(0))
            while pend:
                swa_b(*pend.pop(0))

    tabstate = {"cur": -1}

    def mix1_phase(s):
        S.barrier()
        AR.reset()
        sets = []
        for b_ in range(2):
            Qh_ = AR.alloc([128, T], BF16, f"Qh{b_}")
            KhM_ = [AR.alloc([128, TT], BF16, f"KhM{b_}{k}") for k in range(2)]
            Vz_ = AR.alloc([128, NT, 2, 128], BF16, f"Vz{b_}")
            S.op("pool", lambda e, KhM_=KhM_: e.memset(KhM_[0][64:128, :], 0.0), writes=[KhM_[0]])
            S.op("pool", lambda e, KhM_=KhM_: e.memset(KhM_[1][0:64, :], 0.0), writes=[KhM_[1]])
            S.op("pool", lambda e, Vz_=Vz_: e.memset(Vz_[:], 0.0), writes=[Vz_])
            sets.append((Qh_, KhM_, Vz_))
        PTt = [AR.alloc([128, 7, 2, 128], BF16, f"PT{k}") for k in range(3)]
        PTb = [[Buf() for _ in range(7)] for k in range(3)]
        lt = [AR.alloc([128, 2, 128], F32, f"lt{k}") for k in range(4)]
        dn = [AR.alloc([128, 128], F32, f"dn{k}") for k in range(3)]
        NA_DEPTH = dbg.get("na_depth", 2)
        if tabstate["cur"] != 1:
            S.dma("pool", lambda e: e.dma_start(out=rpbT[:], in_=rpbg_in.rearrange("p (o c x) -> p o c x", o=7, c=8)), writes=[rpbT])
            S.dma("pool", lambda e: e.dma_start(out=namT[:], in_=nam_in), writes=[namT])
            tabstate["cur"] = 1
        WB4 = AR.alloc([128, KC, 384], BF16, "WB4")
        WBn = [WB, WB4]

        def proj_units(WBc, Qh, KhM, Vz):
            units = []
            for (t0, n) in SUP[:4]:
                def uq(t0=t0, n=n):
                    pq = bank()
                    projT(pq, WBc, 0, t0, n)
                    S.op("act", lambda e: e.copy(out=Qh[:, t0:t0 + n], in_=pq[:, 0:n]), reads=[pq], writes=[Qh])
                units.append(uq)
            for (t0, n) in SUP:
                def uk(t0=t0, n=n):
                    pk = bank()
                    projT(pk, WBc, 128, t0, n)
                    S.op("dve", lambda e: e.tensor_copy(out=KhM[1][64:128, t0:t0 + n], in_=pk[64:128, 0:n]), reads=[pk], writes=[KhM[1]])
                    S.op("act", lambda e: e.copy(out=KhM[0][0:64, t0:t0 + n], in_=pk[0:64, 0:n]), reads=[pk], writes=[KhM[0]])
                units.append(uk)
            for i0 in range(0, NT, 4):
                def uv(i0=i0):
                    ni = min(4, NT - i0)
                    pv = bank()
                    for k in range(ni):
                        projTok(pv[:, k * 128:(k + 1) * 128], pv, WBc, 256, 128, i0 + k)
                    pvv = pv[:, 0:ni * 128].rearrange("p (a b) -> p a b", b=128)
                    S.op("act", lambda e: e.copy(out=Vz[:, i0:i0 + ni, 0, 0:64], in_=pvv[:, :, 0:64]), reads=[pv], writes=[Vz])
                    S.op("dve", lambda e: e.tensor_copy(out=Vz[:, i0:i0 + ni, 1, 64:128], in_=pvv[:, :, 64:128]), reads=[pv], writes=[Vz])
                units.append(uv)
            return units

        def na_a(hc, Qh, KhM, it, Pt, Pb):
            kts = [(a, o_, pid) for (a, o_, pid) in na_plan[it]] + [(16, None, None), (17, None, None)]
            qs = slice(it * 128, (it + 1) * 128)
            for k, (kt, o_, pid) in enumerate(kts):
                ks = slice(kt * 128, (kt + 1) * 128)
                if o_ is not None:
                    ps = bank()
                    for hh in range(2):
                        S.op("pe", lambda e, ps=ps, hh=hh, ks=ks: e.matmul(ps[:, hh * 128:(hh + 1) * 128], lhsT=KhM[hh][:, ks], rhs=Qh[:, qs], start=True, stop=True),
                             reads=[KhM[hh], Qh], writes=[ps])
                    ltt = lt[k % 4]
                    S.op("dve", lambda e, ps=ps, ltt=ltt, o_=o_: e.scalar_tensor_tensor(
                        out=ltt[:], in0=ps[:, 0:256].rearrange("p (h b) -> p h b", b=128), scalar=0.125,
                        in1=rpbT[:, o_ + 3, hc, :].rearrange("p (h b) -> p h b", b=128), op0=ALU.mult, op1=ALU.add), reads=[ps, rpbT], writes=[ltt])
                    S.op("act", lambda e, ltt=ltt, k=k: e.activation(out=Pt[:, k, :, :], in_=ltt[:], func=AF.Exp), reads=[ltt], writes=[Pb[k]])
                    S.op("pool", lambda e, k=k, pid=pid: e.tensor_tensor(out=Pt[:, k, :, :], in0=Pt[:, k, :, :],
                                                                      in1=namT[:, pid, :].unsqueeze(1).to_broadcast([128, 2, 128]), op=ALU.mult),
                         reads=[Pb[k], namT], writes=[Pb[k]])
            ps = bank()
            kc0 = len(kts) - 2
            for k2 in range(2):
                ks = slice((16 + k2) * 128, (17 + k2) * 128)
                for hh in range(2):
                    S.op("pe", lambda e, ps=ps, k2=k2, hh=hh, ks=ks: e.matmul(
                        ps[:, (k2 * 2 + hh) * 128:(k2 * 2 + hh + 1) * 128], lhsT=KhM[hh][:, ks], rhs=Qh[:, qs], start=True, stop=True), reads=[KhM[hh], Qh], writes=[ps])
            S.op("act", lambda e, ps=ps: e.activation(out=Pt[:, kc0:kc0 + 2, :, :].rearrange("p a h b -> p (a h b)"), in_=ps[:, 0:512],
                                                      func=AF.Exp, scale=0.125), reads=[ps], writes=[Pb[kc0], Pb[kc0 + 1]])
            return kts

        def na_b(hc, Vz, it, Pt, Pb, dnt, kts):
            qs = slice(it * 128, (it + 1) * 128)
            po = bank()
            nk = len(kts)
            for k, (kt, o_, pid) in enumerate(kts):
                for hh in range(2):
                    S.op("pe", lambda e, k=k, kt=kt, hh=hh: e.matmul(
                        po[:, 0:128], lhsT=Vz[:, kt, hh, :], rhs=Pt[:, k, hh, :], start=(k == 0 and hh == 0), stop=(k == nk - 1 and hh == 1)),
                        reads=[Vz, Pb[k]], writes=[po])
            for k, (kt, o_, pid) in enumerate(kts):
                for hh in range(2):
                    sel = selL if hh == 0 else selR
                    S.op("pe", lambda e, k=k, hh=hh, sel=sel: e.matmul(
                        po[:, 128:256], lhsT=sel[:], rhs=Pt[:, k, hh, :], start=(k == 0 and hh == 0), stop=(k == nk - 1 and hh == 1)),
                        reads=[sel, Pb[k]], writes=[po])
            S.op("dve", lambda e: e.reciprocal(out=dnt[:], in_=po[:, 128:256]), reads=[po], writes=[dnt])
            S.op("dve", lambda e: e.tensor_tensor(out=YT[:, hc, qs], in0=po[:, 0:128], in1=dnt[:], op=ALU.mult),
                 reads=[po, dnt], writes=[ybuf[it]])

        load_w("sp", WBn[0], WBn[0][:, :, 0:384], s_win1, 0, 384)
        for u in proj_units(WBn[0], *sets[0]):
            u()
        cnt = 0
        for hc in range(8):
            Qh, KhM, Vz = sets[hc % 2]
            nxt = []
            if hc + 1 < 8:
                load_w("sp", WBn[(hc + 1) % 2], WBn[(hc + 1) % 2][:, :, 0:384], s_win1, (hc + 1) * 384, 384)
                nxt = proj_units(WBn[(hc + 1) % 2], *sets[(hc + 1) % 2])
            pend = []
            for it in range(16):
                Pt, Pb, dnt = PTt[cnt % 3], PTb[cnt % 3], dn[cnt % 3]
                cnt += 1
                kts = na_a(hc, Qh, KhM, it, Pt, Pb)
                pend.append((hc, Vz, it, Pt, Pb, dnt, kts))
                if len(pend) > NA_DEPTH:
                    na_b(*pend.pop(0))
                if nxt and it >= 1:
                    nxt.pop(0)()
            while pend:
                na_b(*pend.pop(0))
            while nxt:
                nxt.pop(0)()

    def residual(s, i, psA, psB, GG, X, Xn, T1, ss):
        S.op("act", lambda e: e.activation(out=T1[:, 0:512], in_=psA[:, :], func=AF.Square, accum_out=ss[:, 0:1]), reads=[psA], writes=[T1, ss])
        S.op("act", lambda e: e.activation(out=T1[:, 512:1024], in_=psB[:, :], func=AF.Square, accum_out=ss[:, 1:2]), reads=[psB], writes=[T1, ss])
        S.op("dve", lambda e: e.tensor_tensor(out=ss[:, 0:1], in0=ss[:, 0:1], in1=ss[:, 1:2], op=ALU.add), reads=[ss], writes=[ss])
        S.op("act", lambda e: e.activation(out=ss[:, 1:2], in_=ss[:, 0:1], func=AF.Sqrt, bias=EPS, scale=1.0 / D), reads=[ss], writes=[ss])
        S.op("dve", lambda e: e.reciprocal(out=ss[:, 2:3], in_=ss[:, 1:2]), reads=[ss], writes=[ss])
        S.op("dve", lambda e: e.tensor_tensor(out=T1[:, 0:512], in0=psA[:, :], in1=GG[:, 0:512], op=ALU.mult), reads=[psA, GG], writes=[T1])
        S.op("dve", lambda e: e.tensor_tensor(out=T1[:, 512:1024], in0=psB[:, :], in1=GG[:, 512:1024], op=ALU.mult), reads=[psB, GG], writes=[T1])
        S.op("dve", lambda e: e.scalar_tensor_tensor(out=Xn[:], in0=T1[:], scalar=ss[:, 2:3], in1=X[:], op0=ALU.mult, op1=ALU.add),
             reads=[T1, ss, X], writes=[Xn])
        S.dma("sp", lambda e: e.dma_start(out=x_dst(s, i), in_=Xn[:]), reads=[Xn], writes=[xb(s, i)])

    def load_gg(GG, l, wi, b):
        S.dma("sp", lambda e: e.dma_start(out=GG[:], in_=ggd[l, wi, b].partition_broadcast(128)), reads=[ggbuf[(l, wi, b)]], writes=[GG])

    def o1_phase(l, s, first, ntiles):
        S.barrier()
        AR.reset()
        Wo = AR.alloc([128, KC, D], BF16, "Wo")
        GGx = AR.alloc([128, D], F32, "GGx")
        GGc = AR.alloc([128, D], F32, "GGc")
        NBUF = 3
        Xs = [AR.alloc([128, D], F32, f"X{k}") for k in range(NBUF)]
        T1 = [AR.alloc([128, D], F32, f"T{k}") for k in range(NBUF)]
        junk = AR.alloc([128, D], BF16, "junk")
        sc = [dict(ss=AR.alloc([128, 4], F32, f"ss{k}"), junk=junk, xs=AR.alloc([128, D], BF16, f"xs{k}"), tmp=AR.alloc([128, D], F32, f"tmp{k}"))
              for k in range(NBUF)]
        ssr = [AR.alloc([128, 4], F32, f"ssr{k}") for k in range(NBUF)]
        load_w("sp", Wo, Wo[:], s_wout0 if l == 0 else s_wout1, 0, D)
        load_gg(GGx, l, 0, s)
        if ntiles > 16:
            load_gg(GGc, l, 0, NB - 1)

        def st1(i):
            X = Xs[i % NBUF]
            S.dma("sp", lambda e, X=X, i=i: e.dma_start(out=X[:], in_=x_ap(l, s, i, first)), reads=[xb(s, i)], writes=[X])
            pa, pb_ = bank(), bank()
            for hf, ps in enumerate((pa, pb_)):
                for kc in range(KC):
                    S.op("pe", lambda e, ps=ps, kc=kc, hf=hf, i=i: e.matmul(ps[:, :], lhsT=YT[:, kc, i * 128:(i + 1) * 128], rhs=Wo[:, kc, hf * 512:(hf + 1) * 512],
                                                                           start=(kc == 0), stop=(kc == KC - 1)), reads=[ybuf[i], Wo], writes=[ps])
            return (i, pa, pb_)

        def st2(i, pa, pb_):
            k = i % NBUF
            residual(s, i, pa, pb_, GGx if i < 16 else GGc, Xs[k], T1[k], T1[k], ssr[k])
            pt = norm_a(T1[k], i, sc[k])
            b = s if i < 16 else NB - 1
            return (pt, i, DER["A2"][:, b, :], DER["B2"][:, b, :], sc[k])

        q1, q2 = [], []
        for i in range(ntiles):
            q1.append(st1(i))
            if len(q1) > 1:
                q2.append(st2(*q1.pop(0)))
            if len(q2) > 1:
                norm_b(*q2.pop(0))
        while q1:
            q2.append(st2(*q1.pop(0)))
            if len(q2) > 1:
                norm_b(*q2.pop(0))
        while q2:
            norm_b(*q2.pop(0))

    def ffn_phase(l, s, ntiles):
        S.barrier()
        AR.reset()
        mT = AR.alloc([128, NJ, 512], BF16, "mT")
        Wu = [[AR.alloc([128, KC, 256], BF16, f"wu{a}{k}") for k in range(2)] + [WUX[a]] for a in range(2)]
        GGx = AR.alloc([128, D], F32, "GGx")
        Xs = [AR.alloc([128, D], F32, f"X{k}") for k in range(2)]
        Xn = Xs
        T1 = [AR.alloc([128, D], F32, f"T{k}") for k in range(2)]
        ssr = [AR.alloc([128, 4], F32, f"ssr{k}") for k in range(2)]
        ua = [AR.alloc([128, 2, 256], F32, f"ua{k}") for k in range(2)]
        ug = [AR.alloc([128, 2, 256], F32, f"ug{k}") for k in range(2)]
        sg = [AR.alloc([128, 2, 256], BF16, f"sg{k}") for k in range(2)]
        load_gg(GGx, l, 1, s)
        S.dma("act", lambda e: e.dma_start(out=WD[:], in_=s_wdown[l].ap.rearrange("(j p) n -> p j n", p=128)), reads=s_wdown[l].bufs, writes=[WD])
        sups = SUP[:4] + ([SUP[4]] if ntiles > 16 else [])
        cnt = 0
        nblk = len(sups) * 11

        def wload(q):
            if q >= nblk:
                return
            jb = q % 11
            wa, wg = Wu[0][q % 3], Wu[1][q % 3]
            S.dma("sp", lambda e, wa=wa, jb=jb: e.dma_start(out=wa[:], in_=wups[l][jb].rearrange("p (kc n) -> p kc n", kc=KC)),
                  reads=[s_wup[l].bufs[jb]], writes=[wa])
            S.dma("sp", lambda e, wg=wg, jb=jb: e.dma_start(out=wg[:], in_=wups[l][11 + jb].rearrange("p (kc n) -> p kc n", kc=KC)),
                  reads=[s_wup[l].bufs[11 + jb]], writes=[wg])

        wload(0)
        wload(1)
        for si, (s0, sn) in enumerate(sups):
            nh = sn // 256
            if s0 >= T:
                load_gg(GGx, l, 1, NB - 1)
            for jb in range(11):
                q = si * 11 + jb
                wa, wg = Wu[0][q % 3], Wu[1][q % 3]
                wload(q + 2)
                for cc in range(2):
                    j = jb * 2 + cc
                    pa, pg = pbank(), pbank()
                    for hf in range(nh):
                        t0 = s0 + hf * 256
                        c0 = hcol(t0) - 1
                        tiles = [hbuf[i] for i in range(max(t0 // 128 - 1, 0), min(t0 // 128 + 3, NT))]
                        for (pp, w) in ((pa, wa), (pg, wg)):
                            for kc in range(KC):
                                S.op("pe", lambda e, pp=pp, w=w, kc=kc, hf=hf, c0=c0, cc=cc: e.matmul(
                                    pp[:, hf * 512:hf * 512 + 258], lhsT=w[:, kc, cc * 128:(cc + 1) * 128], rhs=hT[:, kc, c0:c0 + 258],
                                    start=(kc == 0), stop=(kc == KC - 1)), reads=[w] + tiles, writes=[pp])
                    k = cnt % 2
                    cnt += 1

                    def pv(pp, o_):
                        return pp[:, :].rearrange("p (a b) -> p a b", a=2)[:, 0:nh, o_:o_ + 256]

                    paths = ((pa, ua[k], j), (pg, ug[k], NJ + j))
                    for (pp, u, jj) in paths:
                        S.op("act", lambda e, pp=pp, u=u, jj=jj, nh=nh: e.activation(out=u[:, 0:nh, :], in_=pp[:, :].rearrange("p (a b) -> p a b", a=2)[:, 0:nh, 1:257],
                                                                                   func=AF.Identity, bias=cbT[:, jj:jj + 1], scale=cwT[:, 1, jj:jj + 1]),
                             reads=[pp, cbT, cwT], writes=[u])
                    for tap, o_ in ((0, 0), (2, 2)):
                        for (pp, u, jj) in paths:
                            S.op("dve", lambda e, pp=pp, u=u, jj=jj, nh=nh, tap=tap, o_=o_: e.scalar_tensor_tensor(
                                out=u[:, 0:nh, :], in0=pp[:, :].rearrange("p (a b) -> p a b", a=2)[:, 0:nh, o_:o_ + 256], scalar=cwT[:, tap, jj:jj + 1],
                                in1=u[:, 0:nh, :], op0=ALU.mult, op1=ALU.add), reads=[pp, cwT, u], writes=[u])
                    S.op("act", lambda e, k=k, nh=nh: e.activation(out=sg[k][:, 0:nh, :], in_=ug[k][:, 0:nh, :], func=AF.Silu), reads=[ug[k]], writes=[sg[k]])
                    S.op("pool", lambda e, k=k, j=j, nh=nh, sn=sn: e.tensor_tensor(out=mT[:, j, 0:sn].rearrange("p (a b) -> p a b", b=256), in0=sg[k][:, 0:nh, :],
                                                                                 in1=ua[k][:, 0:nh, :], op=ALU.mult), reads=[sg[k], ua[k]], writes=[mT])
            for it in range(sn // 128):
                i = s0 // 128 + it
                X = Xs[i % 2]
                S.dma("sp", lambda e, X=X, i=i: e.dma_start(out=X[:], in_=x_ap(l, s, i, False)), reads=[xb(s, i)], writes=[X])
                pp = pbank()
                pa = PView(pp.base[:, 0:512], pp.buf)
                pb_ = PView(pp.base[:, 512:1024], pp.buf)
                for hf, ps in enumerate((pa, pb_)):
                    for j in range(NJ):
                        S.op("pe", lambda e, ps=ps, j=j, hf=hf, it=it: e.matmul(ps[:, :], lhsT=mT[:, j, it * 128:(it + 1) * 128], rhs=WD[:, j, hf * 512:(hf + 1) * 512],
                                                                                start=(j == 0), stop=(j == NJ - 1)), reads=[mT, WD], writes=[ps])
                residual(s, i, pa, pb_, GGx, X, Xn[i % 2], T1[i % 2], ssr[i % 2])

    stop = dbg.get("stop", 99)
    for l in layers:
        if stop <= 1:
            break
        mod_phase(l)
        if stop <= 2:
            break
        last = l == 1
        nt = 16 if last else NT
        for s in range(NSEQ):
            n1_phase(l, s, True)
            if l == first_layer:
                nd = (len(deferred) + NSEQ - 1 - s) // (NSEQ - s) if deferred else 0
                for _ in range(nd):
                    stage_layer(*deferred.pop(0))
            if stop <= 3:
                break
            if l == 0:
                mix0_phase(s)
            else:
                mix1_phase(s)
            if stop <= 5:
                break
            if dbg.get("yt") == l and s == 0:
                S.barrier()
                AR.reset()
                dy = dbg_out("yt", [128, KC * TT])
                ytf = AR.alloc([128, KC * TT // 2], F32, "ytf")
                for hf in range(2):
                    S.op("dve", lambda e, hf=hf: e.tensor_copy(out=ytf[:], in_=YT[:].rearrange("p c t -> p (c t)")[:, hf * KC * TT // 2:(hf + 1) * KC * TT // 2]),
                         reads=ybuf, writes=[ytf])
                    S.dma("sp", lambda e, hf=hf: e.dma_start(out=dy[:, hf * KC * TT // 2:(hf + 1) * KC * TT // 2], in_=ytf[:]), reads=[ytf])
            o1_phase(l, s, True, nt)
            if stop <= 6:
                break
            ffn_phase(l, s, nt)
    S.barrier(final=True)
    S.emit()
    return nc, S


_CACHE = {}


def _host_layout(inputs, NSEQ, core):
    f = lambda a: np.ascontiguousarray(np.asarray(a, dtype=np.float32))
    b0 = core * NSEQ
    c = np.concatenate([np.asarray(inputs["c"])[b0:b0 + NSEQ], np.asarray(inputs["c_ctx"])[None, :]], axis=0)
    NB = NSEQ + 1
    m = {
        "x": f(np.asarray(inputs["x"])[b0:b0 + NSEQ]),
        "ctx": f(np.asarray(inputs["ctx"])[b0:b0 + NSEQ]),
        "cT": f(c.reshape(NB, KC, 128).transpose(2, 1, 0)),
    }
    return m


def _shared_layout(inputs):
    f = lambda a: np.ascontiguousarray(np.asarray(a, dtype=np.float32))
    key = "shared"
    plan, pats = _na_geometry()
    ret_tab, ret_cols = _ret_tables()
    m = {
        "w_mod": f(inputs["w_mod"]),
        "b_modT": f(np.asarray(inputs["b_mod"]).reshape(2, 48, 128).transpose(0, 2, 1)),
        "norm_gT": f(np.asarray(inputs["norm_g"]).reshape(2, 4, 8, 128).transpose(0, 3, 1, 2)),
        "ab_w_in": f(np.asarray(inputs["ab_w_in"])[0]),
        "ab_w_out": f(np.asarray(inputs["ab_w_out"])[0]),
        "ret_decay": f(np.asarray(inputs["ret_decay_exp"]).reshape(1, 8)),
        "swa_sink": f(np.asarray(inputs["swa_sink"]).reshape(1, 8)),
        "na_w_in": f(np.asarray(inputs["na_w_in"])[0]),
        "na_w_out": f(np.asarray(inputs["na_w_out"])[0]),
        "rpb_g": f(_rpb_gather(np.asarray(inputs["na_rpb"], dtype=np.float32)[0]).reshape(128, -1)),
        "ffn_w_up": f(inputs["ffn_w_up"]),
        "conv_wT": f(np.asarray(inputs["ffn_conv_w"]).reshape(2, 3, 44, 128).transpose(0, 3, 1, 2)),
        "conv_bT": f(np.asarray(inputs["ffn_conv_b"]).reshape(2, 44, 128).transpose(0, 2, 1)),
        "ffn_w_down": f(inputs["ffn_w_down"]),
        "rope_tab": _rope_tables(),
        "ret_tab": f(ret_tab),
        "ret_cols": f(ret_cols),
        "swa_mask": f(_swa_masks()),
        "na_mask": f(pats.transpose(1, 0, 2)),
    }
    return m, plan, pats.shape[0]


def kernel(**inputs):
    n_cores = 8
    B = np.asarray(inputs["x"]).shape[0]
    NSEQ = B // n_cores
    shared, plan, npat = _shared_layout(inputs)
    if "nc" not in _CACHE:
        _CACHE["nc"] = build_program(NSEQ, plan, npat)[0]
    nc = _CACHE["nc"]
    in_maps = []
    for core in range(n_cores):
        m = dict(shared)
        m.update(_host_layout(inputs, NSEQ, core))
        in_maps.append(m)
    res = run_bass_kernel_spmd(nc, in_maps, core_ids=list(range(n_cores)))
    return np.concatenate([np.asarray(r["out"], dtype=np.float32) for r in res.results], axis=0)
```

```python
import numpy as np
import concourse.bass as bass
import concourse.mybir as mybir
from concourse.bass_utils import run_bass_kernel_spmd

F32 = mybir.dt.float32
BF16 = mybir.dt.bfloat16
AF = mybir.ActivationFunctionType
ALU = mybir.AluOpType

T = 2048
L = 256
TT = T + L
NT = 18
D = 1024
KC = 8
DFF = 2816
NJ = 22
EPS = 1e-6
LN2 = 0.6931471805599453


class Buf:
    __slots__ = ("name", "w", "r")

    def __init__(self, name=""):
        self.name = name
        self.w = None
        self.r = []


class Tile:
    def __init__(self, t, name=""):
        self.t = t
        self.buf = Buf(name)

    def __getitem__(self, k):
        return self.t[k]


def _bufs(xs):
    out = []
    for x in xs:
        if isinstance(x, Buf):
            out.append(x)
        elif isinstance(x, (list, tuple)):
            out.extend(_bufs(x))
        else:
            out.append(x.buf)
    return out


EPOCH = 30000
COMPUTE = ("pe", "act", "dve", "pool")
QUEUES = ("sp", "act", "pool")


class Sched:
    def __init__(self, nc, n_dma_sems=16):
        self.nc = nc
        self.stream = {e: [] for e in ("pe", "act", "dve", "pool", "sp")}
        self.cnt = {e: 0 for e in COMPUTE}
        self.sems = {}
        self.seen = {e: {} for e in self.stream}
        self.n_dma_sems = n_dma_sems
        self.dma_rr = {q: 0 for q in QUEUES}
        self.dma_cnt = {}
        self.n_ops = 0

    def sem(self, key):
        if key not in self.sems:
            self.sems[key] = self.nc.alloc_semaphore("s_" + "_".join(str(k) for k in key))
        return self.sems[key]

    def _need(self, eng, tok, waits, raw):
        if tok is None:
            return
        key, val, teng, is_dma = tok
        if not is_dma and teng == eng:
            if eng == "pe" or not raw:
                return
        if self.seen[eng].get(key, 0) >= val:
            return
        if waits.get(key, 0) < val:
            waits[key] = val

    def _deps(self, eng, reads, writes):
        waits = {}
        for b in reads:
            self._need(eng, b.w, waits, True)
        for b in writes:
            self._need(eng, b.w, waits, False)
            for t in b.r:
                self._need(eng, t, waits, False)
        return waits

    def _commit(self, tok, reads, writes):
        for b in reads:
            b.r.append(tok)
        for b in writes:
            b.w = tok
            b.r = []

    def _cur_tok(self, e):
        c = self.cnt[e]
        if not c:
            return None
        return ((e, (c - 1) // EPOCH), (c - 1) % EPOCH + 1, e, False)

    def op(self, eng, fn, reads=(), writes=()):
        reads = _bufs(reads)
        writes = _bufs(writes)
        waits = self._deps(eng, reads, writes)
        for key, val in waits.items():
            self.stream[eng].append(("wait", key, val))
            self.seen[eng][key] = val
        self.cnt[eng] += 1
        tok = self._cur_tok(eng)
        self.sem(tok[0])
        self.stream[eng].append(("op", fn, tok[0], 1))
        self._commit(tok, reads, writes)
        self.n_ops += 1
        return tok

    def dma(self, q, fn, reads=(), writes=()):
        reads = _bufs(reads)
        writes = _bufs(writes)
        waits = self._deps(q, reads, writes)
        i = self.dma_rr[q] % self.n_dma_sems
        self.dma_rr[q] += 1
        key = ("d" + q, i)
        self.sem(key)
        prev = self.dma_cnt.get(key, 0)
        if prev and self.seen[q].get(key, 0) < prev:
            waits[key] = max(waits.get(key, 0), prev)
        for k, val in waits.items():
            self.stream[q].append(("wait", k, val))
            self.seen[q][k] = val
        val = prev + 16
        self.dma_cnt[key] = val
        self.stream[q].append(("op", fn, key, 16))
        tok = (key, val, q, True)
        self._commit(tok, reads, writes)
        self.n_ops += 1
        return tok

    def barrier(self, final=False):
        toks = [self._cur_tok(e) for e in COMPUTE]
        toks = [t for t in toks if t is not None]
        dtoks = [(k, v, k[0][1:], True) for k, v in self.dma_cnt.items() if final or k[0] != "dpool"]
        for e in self.stream:
            for tok in toks + dtoks:
                key, val, teng, is_dma = tok
                if not is_dma and teng == e:
                    continue
                if self.seen[e].get(key, 0) >= val:
                    continue
                self.stream[e].append(("wait", key, val))
                self.seen[e][key] = val

    def emit(self):
        nc = self.nc
        sems = self.sems

        def run(eng_name):
            def body(eng):
                for it in self.stream[eng_name]:
                    if it[0] == "wait":
                        eng.wait_ge(sems[it[1]], it[2])
                    else:
                        it[1](eng).then_inc(sems[it[2]], it[3])
            return body

        with nc.Block() as block:
            block.tensor(run("pe"))
            block.scalar(run("act"))
            block.vector(run("dve"))
            block.gpsimd(run("pool"))
            block.sync(run("sp"))


class Arena:
    def __init__(self, nc, base, size, tag):
        self.nc, self.base, self.size, self.tag = nc, base, size, tag
        self.off = 0
        self.n = 0

    def reset(self):
        self.off = 0

    def alloc(self, shape, dtype, name="t"):
        nbytes = int(np.prod(shape[1:])) * (2 if dtype == BF16 else 4)
        nbytes = (nbytes + 63) // 64 * 64
        assert self.off + nbytes <= self.size, (self.tag, name, self.off, nbytes, self.size)
        self.n += 1
        t = self.nc.alloc_sbuf_tensor_at(f"{self.tag}{self.n}_{name}", list(shape), dtype, offset=self.base + self.off)
        self.off += nbytes
        return Tile(t, name)


def _rope_tables():
    t = np.arange(T, dtype=np.float32)
    p = np.arange(128)
    inv = (10000.0 ** (-np.arange(64, dtype=np.float32) / 64)).astype(np.float32)
    ang = t[None, :] * inv[p % 64][:, None]
    cosA = np.cos(ang).astype(np.float32)
    sinA = np.sin(ang).astype(np.float32)
    sinA = np.where((p < 64)[:, None], -sinA, sinA).astype(np.float32)
    d = p % 64
    blk = d // 32
    idx = d % 32
    inv2 = (10000.0 ** (-np.arange(16, dtype=np.float32) / 16)).astype(np.float32)
    pos = np.where((blk == 0)[:, None], (t // 64)[None, :], (t % 64)[None, :]).astype(np.float32)
    ang2 = pos * inv2[idx % 16][:, None]
    cosB = np.cos(ang2).astype(np.float32)
    sinB = np.sin(ang2).astype(np.float32)
    sinB = np.where((idx < 16)[:, None], -sinB, sinB).astype(np.float32)
    return np.stack([cosA, sinA, cosB, sinB]).astype(np.float32)


def _ret_tables():
    j = np.arange(128, dtype=np.float32)[:, None]
    i = np.arange(128, dtype=np.float32)[None, :]
    P1 = np.maximum(i - j, 0)
    P2 = np.maximum(j - i, 0)
    Mge = (i >= j).astype(np.float32)
    Mlt = (i < j).astype(np.float32)
    IP1 = np.broadcast_to(i + 1, (128, 128))
    IB = np.broadcast_to(128 - i, (128, 128))
    tabs = np.stack([P1, P2, Mge, Mlt, IP1, IB]).astype(np.float32)
    cols = np.stack([127 - j[:, 0], j[:, 0]], axis=1).astype(np.float32)
    return np.ascontiguousarray(tabs.transpose(1, 0, 2)), cols


def _swa_masks():
    jj = np.arange(128)[:, None]
    ii = np.arange(128)[None, :]
    return np.stack([(jj >= ii), (jj <= ii)], axis=1).astype(np.float32)


def _na_geometry():
    rows = 32
    rs = lambda r: int(np.clip(r - 4, 0, rows - 8))
    qstart = np.clip(np.arange(64) - 8, 0, 48)
    colv = (np.arange(64)[None, :] >= qstart[:, None]) & (np.arange(64)[None, :] < qstart[:, None] + 16)
    pats = []
    pat_id = {}
    plan = []
    for m in range(16):
        lo = rs(2 * m) // 2
        hi = (rs(2 * m + 1) + 7) // 2
        lst = []
        for a in range(lo, hi + 1):
            mask = np.zeros((128, 128), np.float32)
            for rq in range(2):
                r = 2 * m + rq
                for rk in range(2):
                    kr = 2 * a + rk
                    if rs(r) <= kr < rs(r) + 8:
                        mask[rk * 64:(rk + 1) * 64, rq * 64:(rq + 1) * 64] = colv.T.astype(np.float32)
            if mask.sum() == 0:
                continue
            key = mask.tobytes()
            if key not in pat_id:
                pat_id[key] = len(pats)
                pats.append(mask)
            lst.append((a, a - m, pat_id[key]))
        plan.append(lst)
    return plan, np.stack(pats)


def _rpb_gather(rpb):
    jr = np.arange(128) // 64
    jc = np.arange(128) % 64
    out = np.zeros((128, 7, 8, 2, 128), np.float32)
    dc = np.clip(jc[:, None] - jc[None, :], -15, 15) + 15
    for o in range(-3, 4):
        dr = np.clip(2 * o + jr[:, None] - jr[None, :] + 7, 0, 14)
        g = rpb[:, dr, dc]
        out[:, o + 3] = g.reshape(8, 2, 128, 128).transpose(2, 0, 1, 3)
    return out


def build_program(NSEQ, na_plan, npat, dbg=None, layers=(0, 1)):
    dbg = dbg or {}
    NB = NSEQ + 1
    nc = bass.Bass("TRN2", target_bir_lowering=False)
    S = Sched(nc)

    def din(name, shape, dt=F32):
        return nc.dram_tensor(name, list(shape), dt, kind="ExternalInput").ap()

    x_in = din("x", [NSEQ, T, D])
    ctx_in = din("ctx", [NSEQ, L, D])
    cT_in = din("cT", [128, KC, NB])
    wmod_in = din("w_mod", [2, D, 6 * D])
    bmod_in = din("b_modT", [2, 128, 48])
    ng_in = din("norm_gT", [2, 128, 4, 8])
    abwin_in = din("ab_w_in", [D, 2816])
    abwout_in = din("ab_w_out", [D, D])
    decay_in = din("ret_decay", [1, 8])
    sink_in = din("swa_sink", [1, 8])
    nawin_in = din("na_w_in", [D, 3072])
    nawout_in = din("na_w_out", [D, D])
    rpbg_in = din("rpb_g", [128, 7 * 8 * 256])
    wup_in = din("ffn_w_up", [2, D, 2 * DFF])
    cw_in = din("conv_wT", [2, 128, 3, 44])
    cb_in = din("conv_bT", [2, 128, 44])
    wdown_in = din("ffn_w_down", [2, DFF, D])
    rope_in = din("rope_tab", [4, 128, T])
    rett_in = din("ret_tab", [128, 6, 128])
    retc_in = din("ret_cols", [128, 2])
    swam_in = din("swa_mask", [128, 2, 128])
    nam_in = din("na_mask", [128, npat, 128])
    out = nc.dram_tensor("out", [NSEQ, T, D], F32, kind="ExternalOutput").ap()

    def dscr(name, shape, dt):
        return nc.dram_tensor(name, list(shape), dt, kind="Internal").ap()

    ctxcur = dscr("ctxcur", [NSEQ, L, D], F32)
    ggd = dscr("ggd", [2, 2, NB, D], F32)
    NW0 = 4736
    win0s = dscr("win0s", [D, NW0], BF16)
    wout0s = dscr("wout0s", [D, D], BF16)
    win1s = dscr("win1s", [D, 3072], BF16)
    wout1s = dscr("wout1s", [D, D], BF16)
    wups = [dscr(f"wups{l}", [22, 128, KC * 256], BF16) for l in range(2)]
    wdowns = [dscr(f"wdowns{l}", [DFF, D], BF16) for l in range(2)]
    wmods = [dscr(f"wmods{l}", [D, 6 * D], BF16) for l in range(2)]
    dbg_outs = {}

    def dbg_out(name, shape):
        dbg_outs[name] = nc.dram_tensor("dbg_" + name, list(shape), F32, kind="ExternalOutput").ap()
        return dbg_outs[name]

    BASE = 16512
    o = BASE
    HTW = TT + 4

    def hcol(t):
        return t + 1 if t < T else t + 3

    hT = Tile(nc.alloc_sbuf_tensor_at("hT", [128, KC, HTW], BF16, offset=o))
    o += (KC * HTW * 2 + 63) // 64 * 64
    BIG_BASE = o
    BIG_SIZE = 53248
    YT = Tile(nc.alloc_sbuf_tensor_at("YT", [128, KC, TT], BF16, offset=BIG_BASE))
    WB = Tile(nc.alloc_sbuf_tensor_at("WB", [128, KC, 832], BF16, offset=BIG_BASE + KC * TT * 2))
    WD = Tile(nc.alloc_sbuf_tensor_at("WD", [128, NJ, D], BF16, offset=BIG_BASE))
    WUX = [Tile(nc.alloc_sbuf_tensor_at(f"WUX{k}", [128, KC, 256], BF16, offset=BIG_BASE + NJ * D * 2 + k * 4096)) for k in range(2)]
    o += BIG_SIZE
    TAB_BASE = o
    ropeT = Tile(nc.alloc_sbuf_tensor_at("ropeT", [128, 4, T], F32, offset=TAB_BASE))
    rpbT = Tile(nc.alloc_sbuf_tensor_at("rpbT", [128, 7, 8, 256], BF16, offset=TAB_BASE))
    namT = Tile(nc.alloc_sbuf_tensor_at("namT", [128, npat, 128], BF16, offset=TAB_BASE + 28672))
    assert npat <= 16
    o += 32768
    CA = Arena(nc, o, 16640, "c")
    o += 16640
    AR = Arena(nc, o, 229344 - o, "a")
    hbuf = [Buf(f"hT{i}") for i in range(NT)]
    ybuf = [Buf(f"YT{i}") for i in range(NT)]

    class PView:
        def __init__(self, base, buf=None):
            self.base = base
            self.buf = buf if buf is not None else Buf()

        def __getitem__(self, k):
            return self.base[k]

    PPt = [nc.alloc_psum_tensor(f"pp{i}", [128, 1024], F32) for i in range(4)]
    PB = [PView(PPt[i // 2][:, (i % 2) * 512:(i % 2 + 1) * 512]) for i in range(6)]
    PT = [PView(PPt[3][:, k * 512:(k + 1) * 512].bitcast(BF16)) for k in range(2)]
    PP = [PView(PPt[i][:, :]) for i in range(4)]
    rr = {"pb": 0, "pt": 0, "pp": 0}

    def pbank():
        rr["pp"] += 1
        return PP[rr["pp"] % 4]

    def bank():
        rr["pb"] += 1
        return PB[rr["pb"] % 6]

    def tbank():
        rr["pt"] += 1
        return PT[rr["pt"] % 2]

    identF = CA.alloc([128, 128], F32, "identF")
    identB = CA.alloc([128, 128], BF16, "identB")
    onesB = CA.alloc([128, 128], BF16, "onesB")
    selL = CA.alloc([128, 128], BF16, "selL")
    selR = CA.alloc([128, 128], BF16, "selR")
    swaM = CA.alloc([128, 2, 128], BF16, "swaM")
    retT = CA.alloc([128, 6, 128], F32, "retT")
    retC = CA.alloc([128, 2], F32, "retC")
    DTt = CA.alloc([128, 4, 128], F32, "DT")
    qdf = CA.alloc([128, 4, 128], F32, "qdf")
    qdb = CA.alloc([128, 4, 128], F32, "qdb")
    kdec = CA.alloc([128, 16], F32, "kdec")
    lg = CA.alloc([128, 8], F32, "lg")
    esink = CA.alloc([128, 4], F32, "esink")
    cTt = CA.alloc([128, KC, NB], F32, "cT")
    siluc = CA.alloc([128, KC, NB], BF16, "siluc")
    MOD = CA.alloc([128, 48, NB], F32, "MOD")
    bmod = CA.alloc([128, 48], F32, "bmod")
    ngT = CA.alloc([128, 4, 8], F32, "ngT")
    DER = {k: CA.alloc([128, NB, 8], F32, k) for k in ("A1", "B1", "G1", "A2", "B2", "G2")}
    cwT = CA.alloc([128, 3, 44], F32, "cwT")
    cbT = CA.alloc([128, 44], F32, "cbT")
    tmpE = CA.alloc([128, 128], F32, "tmpE")
    tmpE2 = CA.alloc([128, 128], F32, "tmpE2")
    row8 = CA.alloc([8, 128], F32, "row8")

    for pc in (0, T + 1, T + 2, HTW - 1):
        S.op("pool", lambda e, pc=pc: e.memset(hT[:, :, pc:pc + 1], 0.0), writes=[hT])
    S.op("pool", lambda e: e.memset(identF[:], 1.0), writes=[identF])
    S.op("pool", lambda e: e.affine_select(out=identF[:], in_=identF[:], pattern=[[-1, 128]], compare_op=ALU.is_equal,
                                           fill=0.0, base=0, channel_multiplier=1), reads=[identF], writes=[identF])
    S.op("dve", lambda e: e.tensor_copy(out=identB[:], in_=identF[:]), reads=[identF], writes=[identB])
    S.op("pool", lambda e: e.memset(onesB[:], 1.0), writes=[onesB])
    S.op("pool", lambda e: e.memset(selL[:], 0.0), writes=[selL])
    S.op("pool", lambda e: e.memset(selL[:, 0:64], 1.0), writes=[selL])
    S.op("pool", lambda e: e.memset(selR[:], 0.0), writes=[selR])
    S.op("pool", lambda e: e.memset(selR[:, 64:128], 1.0), writes=[selR])
    S.dma("pool", lambda e: e.dma_start(out=swaM[:], in_=swam_in), writes=[swaM])
    S.dma("sp", lambda e: e.dma_start(out=retT[:], in_=rett_in), writes=[retT])
    S.dma("sp", lambda e: e.dma_start(out=retC[:], in_=retc_in), writes=[retC])
    S.dma("sp", lambda e: e.dma_start(out=cTt[:], in_=cT_in), writes=[cTt])
    S.dma("sp", lambda e: e.dma_start(out=lg[:], in_=decay_in.partition_broadcast(128)), writes=[lg])
    for col in range(4):
        for half in range(2):
            hh = (col // 2) * 4 + (col % 2) * 2 + half
            S.dma("sp", lambda e, col=col, half=half, hh=hh: e.dma_start(
                out=esink[half * 64:(half + 1) * 64, col:col + 1],
                in_=sink_in[0:1, hh:hh + 1].partition_broadcast(64)), writes=[esink])
    S.op("act", lambda e: e.activation(out=esink[:], in_=esink[:], func=AF.Exp), reads=[esink], writes=[esink])
    S.op("act", lambda e: e.activation(out=siluc[:], in_=cTt[:], func=AF.Silu), reads=[cTt], writes=[siluc])
    S.op("act", lambda e: e.activation(out=lg[:], in_=lg[:], func=AF.Exp, scale=-LN2), reads=[lg], writes=[lg])
    S.op("act", lambda e: e.activation(out=lg[:], in_=lg[:], func=AF.Ln, scale=-1.0, bias=1.0), reads=[lg], writes=[lg])
    for h in range(4):
        lf = lg[:, h:h + 1]
        lb = lg[:, 4 + h:5 + h]
        S.op("act", lambda e, lf=lf: e.activation(out=tmpE[:], in_=retT[:, 0, :], func=AF.Exp, scale=lf), reads=[retT, lg], writes=[tmpE])
        S.op("act", lambda e, lb=lb: e.activation(out=tmpE2[:], in_=retT[:, 1, :], func=AF.Exp, scale=lb), reads=[retT, lg], writes=[tmpE2])
        S.op("dve", lambda e: e.tensor_tensor(out=tmpE[:], in0=tmpE[:], in1=retT[:, 2, :], op=ALU.mult), reads=[tmpE, retT], writes=[tmpE])
        S.op("dve", lambda e: e.tensor_tensor(out=tmpE2[:], in0=tmpE2[:], in1=retT[:, 3, :], op=ALU.mult), reads=[tmpE2, retT], writes=[tmpE2])
        S.op("dve", lambda e, h=h: e.tensor_tensor(out=DTt[:, h, :], in0=tmpE[:], in1=tmpE2[:], op=ALU.add), reads=[tmpE, tmpE2], writes=[DTt])
        S.op("act", lambda e, h=h, lf=lf: e.activation(out=qdf[:, h, :], in_=retT[:, 4, :], func=AF.Exp, scale=lf), reads=[retT, lg], writes=[qdf])
        S.op("act", lambda e, h=h, lb=lb: e.activation(out=qdb[:, h, :], in_=retT[:, 5, :], func=AF.Exp, scale=lb), reads=[retT, lg], writes=[qdb])
        S.op("act", lambda e, h=h, lf=lf: e.activation(out=kdec[:, h:h + 1], in_=retC[:, 0:1], func=AF.Exp, scale=lf), reads=[retC, lg], writes=[kdec])
        S.op("act", lambda e, h=h, lb=lb: e.activation(out=kdec[:, 4 + h:5 + h], in_=retC[:, 1:2], func=AF.Exp, scale=lb), reads=[retC, lg], writes=[kdec])
        S.op("act", lambda e, h=h: e.activation(out=kdec[:, 8 + h:9 + h], in_=lg[:, h:h + 1], func=AF.Exp, scale=128.0), reads=[lg], writes=[kdec])
        S.op("act", lambda e, h=h: e.activation(out=kdec[:, 12 + h:13 + h], in_=lg[:, 4 + h:5 + h], func=AF.Exp, scale=128.0), reads=[lg], writes=[kdec])

    class Staged:
        def __init__(self, ap):
            self.ap = ap
            self.bufs = []

    def stage_cols(st, dst_c, src, src_c, n, rb):
        rows = src.shape[0]
        for r0 in range(0, rows, rb):
            r1 = min(rows, r0 + rb)
            b = Buf()
            st.bufs.append(b)
            S.dma("pool", lambda e, r0=r0, r1=r1: e.dma_start(out=st.ap[r0:r1, dst_c:dst_c + n], in_=src[r0:r1, src_c:src_c + n]), writes=[b])

    def stage_runs(st, dst_c, src, runs):
        for (sc, n) in runs:
            stage_cols(st, dst_c, src, sc, n, dbg.get("rbn", 1024) if n < 128 else dbg.get("rbw", 1024))
            dst_c += n
        return dst_c

    s_win0, s_wout0, s_win1, s_wout1 = Staged(win0s), Staged(wout0s), Staged(win1s), Staged(wout1s)
    s_wup = [Staged(a) for a in wups]
    s_wdown = [Staged(a) for a in wdowns]
    s_wmod = [Staged(a) for a in wmods]
    def stage_layer(l, part):
        if part == 0:
            stage_cols(s_wmod[l], 0, wmod_in[l], 0, 6 * D, dbg.get("rbf", 512))
        elif part == 1 and l == 0:
            c = 0
            for h in range(4):
                qa, ka, va, ga = h * 128, 512 + h * 128, 1024 + h * 128, 1536 + h * 128
                c = stage_runs(s_win0, c, abwin_in, [(qa, 128), (qa + 64, 64), (qa, 64), (ka, 128), (ka + 64, 64), (ka, 64), (ga, 128), (va, 128)])
            for g in range(2):
                def sw64(b0):
                    return [(b0 + 16, 16), (b0, 16), (b0 + 48, 16), (b0 + 32, 16)]
                kb = 2560 + g * 64
                vb = 2688 + g * 64
                runs = []
                for qc in range(2):
                    q0 = 2048 + (4 * g + 2 * qc) * 64
                    runs += [(q0, 128)] + sw64(q0) + sw64(q0 + 64)
                runs += [(kb, 64), (kb, 64)] + sw64(kb) + sw64(kb) + [(vb, 64)]
                c = stage_runs(s_win0, c, abwin_in, runs)
            assert c == NW0
            stage_cols(s_wout0, 0, abwout_in, 0, D, dbg.get("rbf", 512))
        elif part == 1 and l == 1:
            c = 0
            for hc in range(8):
                c = stage_runs(s_win1, c, nawin_in, [(hc * 128, 128), (1024 + hc * 128, 128), (2048 + hc * 128, 128)])
            stage_cols(s_wout1, 0, nawout_in, 0, D, dbg.get("rbf", 512))
        elif part == 2:
            for blk in range(22):
                c0 = (blk * 256) if blk < 11 else (DFF + (blk - 11) * 256)
                b_ = Buf()
                s_wup[l].bufs.append(b_)
                S.dma("pool", lambda e, l=l, blk=blk, c0=c0: e.dma_start(out=wups[l][blk].rearrange("p (kc n) -> p kc n", kc=KC),
                                                                         in_=wup_in[l].rearrange("(kc p) n -> p kc n", p=128)[:, :, c0:c0 + 256]), writes=[b_])
        elif part == 3:
            stage_cols(s_wdown[l], 0, wdown_in[l], 0, D, dbg.get("rbf", 512))

    first_layer = layers[0]
    for part in range(4):
        stage_layer(first_layer, part)
    deferred = [(l, part) for l in layers[1:] for part in range(4)]

    def load_w(q, dst, dst_ap, st, c0, n):
        src = st.ap.rearrange("(kc p) n -> p kc n", p=128)[:, :, c0:c0 + n]
        S.dma(q, lambda e: e.dma_start(out=dst_ap, in_=src), reads=st.bufs, writes=[dst])

    xbuf = {}

    def xb(s, i):
        if (s, i) not in xbuf:
            xbuf[(s, i)] = Buf(f"x{s}_{i}")
        return xbuf[(s, i)]

    def x_ap(l, s, i, first):
        if i < 16:
            src = x_in if (first and l == 0) else out
            return src[s, i * 128:(i + 1) * 128, :]
        src = ctx_in if (first and l == 0) else ctxcur
        return src[s, (i - 16) * 128:(i - 15) * 128, :]

    def x_dst(s, i):
        if i < 16:
            return out[s, i * 128:(i + 1) * 128, :]
        return ctxcur[s, (i - 16) * 128:(i - 15) * 128, :]

    def mod_phase(l):
        S.barrier()
        AR.reset()
        S.dma("sp", lambda e: e.dma_start(out=bmod[:], in_=bmod_in[l]), writes=[bmod])
        S.dma("sp", lambda e: e.dma_start(out=ngT[:], in_=ng_in[l]), writes=[ngT])
        S.dma("sp", lambda e: e.dma_start(out=cwT[:], in_=cw_in[l]), writes=[cwT])
        S.dma("sp", lambda e: e.dma_start(out=cbT[:], in_=cb_in[l]), writes=[cbT])
        Wm = [AR.alloc([128, KC, 512], BF16, f"wm{i}") for i in range(2)]
        psM = bank()
        for blk in range(12):
            w = Wm[blk % 2]
            load_w("sp", w, w[:], s_wmod[l], blk * 512, 512)
            for cc in range(4):
                ch = blk * 4 + cc
                for kc in range(KC):
                    S.op("pe", lambda e, w=w, cc=cc, ch=ch, kc=kc: e.matmul(
                        psM[:, ch * NB:(ch + 1) * NB], lhsT=w[:, kc, cc * 128:(cc + 1) * 128], rhs=siluc[:, kc, :],
                        start=(kc == 0), stop=(kc == KC - 1)), reads=[w, siluc], writes=[psM])
        S.op("dve", lambda e: e.tensor_tensor(out=MOD[:], in0=psM[:, 0:48 * NB].rearrange("p (c b) -> p c b", b=NB),
                                              in1=bmod[:].unsqueeze(2).to_broadcast([128, 48, NB]), op=ALU.add),
             reads=[psM, bmod], writes=[MOD])

        def mv(g):
            return MOD[:, g * 8:(g + 1) * 8, :].rearrange("p c b -> p b c")

        def gb(k):
            return ngT[:, k, :].unsqueeze(1).to_broadcast([128, NB, 8])

        for (nm, gs, gk, plus1) in (("A1", 1, 0, True), ("G1", 2, 1, False), ("A2", 4, 2, True), ("G2", 5, 3, False)):
            d = DER[nm]
            if plus1:
                S.op("dve", lambda e, d=d, gs=gs, gk=gk: e.scalar_tensor_tensor(
                    out=d[:], in0=mv(gs), scalar=1.0, in1=gb(gk), op0=ALU.add, op1=ALU.mult), reads=[MOD, ngT], writes=[d])
            else:
                S.op("dve", lambda e, d=d, gs=gs, gk=gk: e.tensor_tensor(out=d[:], in0=mv(gs), in1=gb(gk), op=ALU.mult),
                     reads=[MOD, ngT], writes=[d])
        S.op("dve", lambda e: e.tensor_copy(out=DER["B1"][:], in_=mv(0)), reads=[MOD], writes=[DER["B1"]])
        S.op("dve", lambda e: e.tensor_copy(out=DER["B2"][:], in_=mv(3)), reads=[MOD], writes=[DER["B2"]])
        for wi, nm in enumerate(("G1", "G2")):
            for b in range(NB):
                ps = bank()
                S.op("pe", lambda e, ps=ps, nm=nm, b=b: e.transpose(out=ps[0:8, 0:128], in_=DER[nm][:, b, :], identity=identF[:]),
                     reads=[DER[nm], identF], writes=[ps])
                S.op("dve", lambda e, ps=ps: e.tensor_copy(out=row8[:], in_=ps[0:8, 0:128]), reads=[ps], writes=[row8])
                gbuf = ggbuf[(l, wi, b)] = Buf()
                S.dma("sp", lambda e, wi=wi, b=b: e.dma_start(out=ggd[l, wi, b].rearrange("(a c) -> a c", c=128), in_=row8[:]),
                      reads=[row8], writes=[gbuf])

    ggbuf = {}

    def norm_a(X, i, tl):
        ss, junk, xs = tl["ss"], tl["junk"], tl["xs"]
        S.op("act", lambda e: e.activation(out=junk[:], in_=X[:], func=AF.Square, accum_out=ss[:, 0:1]), reads=[X], writes=[junk, ss])
        S.op("act", lambda e: e.activation(out=ss[:, 1:2], in_=ss[:, 0:1], func=AF.Sqrt, bias=EPS, scale=1.0 / D), reads=[ss], writes=[ss])
        S.op("dve", lambda e: e.reciprocal(out=ss[:, 2:3], in_=ss[:, 1:2]), reads=[ss], writes=[ss])
        S.op("dve", lambda e: e.tensor_scalar(out=xs[:], in0=X[:], scalar1=ss[:, 2:3], scalar2=None, op0=ALU.mult), reads=[X, ss], writes=[xs])
        pt = tbank()
        for c in range(KC):
            S.op("pe", lambda e, c=c: e.transpose(out=pt[:, c * 128:(c + 1) * 128], in_=xs[:, c * 128:(c + 1) * 128], identity=identB[:]),
                 reads=[xs, identB], writes=[pt])
        return pt

    def norm_b(pt, i, Acol, Bcol, tl):
        tmp = tl["tmp"]
        S.op("dve", lambda e: e.tensor_tensor(out=tmp[:].rearrange("p (c t) -> p c t", t=128), in0=pt[:].rearrange("p (c t) -> p c t", t=128),
                                              in1=Acol.unsqueeze(2).to_broadcast([128, KC, 128]), op=ALU.mult), reads=[pt, DER["A1"], DER["A2"]], writes=[tmp])
        S.op("pool", lambda e: e.tensor_tensor(out=hT[:, :, hcol(i * 128):hcol(i * 128) + 128], in0=tmp[:].rearrange("p (c t) -> p c t", t=128),
                                               in1=Bcol.unsqueeze(2).to_broadcast([128, KC, 128]), op=ALU.add), reads=[tmp, DER["B1"], DER["B2"], hT], writes=[hbuf[i]])

    def norm_scratch(k):
        return dict(ss=AR.alloc([128, 4], F32, f"ss{k}"), junk=AR.alloc([128, D], BF16, f"junk{k}"),
                    xs=AR.alloc([128, D], BF16, f"xs{k}"), tmp=AR.alloc([128, D], F32, f"tmp{k}"))

    def n1_phase(l, s, first):
        S.barrier()
        AR.reset()
        Xs = [AR.alloc([128, D], F32, f"X{k}") for k in range(2)]
        sc = [norm_scratch(k) for k in range(2)]
        pend = []
        for i in range(NT):
            X = Xs[i % 2]
            S.dma("sp", lambda e, X=X, i=i: e.dma_start(out=X[:], in_=x_ap(l, s, i, first)), reads=[xb(s, i)], writes=[X])
            b = s if i < 16 else NB - 1
            pt = norm_a(X, i, sc[i % 2])
            pend.append((pt, i, DER["A1"][:, b, :], DER["B1"][:, b, :], sc[i % 2]))
            if len(pend) > 1:
                norm_b(*pend.pop(0))
        while pend:
            norm_b(*pend.pop(0))

    def projT(ps, w, wc0, t0, n):
        tiles = [hbuf[i] for i in range(t0 // 128, (t0 + n + 127) // 128)]
        for kc in range(KC):
            S.op("pe", lambda e, kc=kc: e.matmul(ps[:, 0:n], lhsT=w[:, kc, wc0:wc0 + 128], rhs=hT[:, kc, hcol(t0):hcol(t0) + n],
                                                 start=(kc == 0), stop=(kc == KC - 1)), reads=[w] + tiles, writes=[ps])

    def projTok(ps_ap, ps, w, wc0, n, i):
        for kc in range(KC):
            S.op("pe", lambda e, kc=kc: e.matmul(ps_ap, lhsT=hT[:, kc, hcol(i * 128):hcol(i * 128) + 128], rhs=w[:, kc, wc0:wc0 + n],
                                                 start=(kc == 0), stop=(kc == KC - 1)), reads=[w, hbuf[i]], writes=[ps])

    SUP = [(0, 512), (512, 512), (1024, 512), (1536, 512), (2048, 256)]

    def rope_proj(w, c_x, c_sw, dst, tabc, tabs, t1, t2):
        for (t0, n) in SUP:
            pq = bank()
            projT(pq, w, c_x, t0, n)
            if t0 < T:
                pqs = bank()
                projT(pqs, w, c_sw, t0, n)
                S.op("dve", lambda e, pq=pq, t0=t0, n=n: e.tensor_tensor(out=t1[:, 0:n], in0=pq[:, 0:n], in1=ropeT[:, tabc, t0:t0 + n], op=ALU.mult),
                     reads=[pq, ropeT], writes=[t1])
                S.op("dve", lambda e, pqs=pqs, t0=t0, n=n: e.tensor_tensor(out=t2[:, 0:n], in0=pqs[:, 0:n], in1=ropeT[:, tabs, t0:t0 + n], op=ALU.mult),
                     reads=[pqs, ropeT], writes=[t2])
                S.op(dbg.get("rope_add", "dve"), lambda e, t0=t0, n=n: e.tensor_tensor(out=dst[:, t0:t0 + n], in0=t1[:, 0:n], in1=t2[:, 0:n], op=ALU.add),
                     reads=[t1, t2], writes=[dst])
            else:
                S.op("act", lambda e, pq=pq, t0=t0, n=n: e.copy(out=dst[:, t0:t0 + n], in_=pq[:, 0:n]), reads=[pq], writes=[dst])

    def mix0_phase(s):
        S.barrier()
        AR.reset()
        t1 = AR.alloc([128, 512], F32, "t1")
        t2 = AR.alloc([128, 512], F32, "t2")
        t3 = t2
        KVf = AR.alloc([128, NT, 128], BF16, "KVf")
        KVb = AR.alloc([128, NT, 128], BF16, "KVb")
        ysq2 = [AR.alloc([128, 512], BF16, f"ysq{k}") for k in range(2)]
        slot = lambda nm, shape=(128, TT): AR.alloc(list(shape), BF16, nm)
        qT, kT, gsT = slot("qT"), slot("kT"), slot("gs")
        Vt = slot("V", (128, NT, 128))
        kf, kb_ = slot("kf", (128, NT, 128)), slot("kb", (128, NT, 128))
        Sfb, Sbb = slot("Sfb", (128, NT, 128)), slot("Sbb", (128, NT, 128))
        inner = [AR.alloc([128, 4, 128], BF16, f"in{k}") for k in range(2)]
        qfT = [AR.alloc([128, 4, 128], BF16, f"qf{k}") for k in range(2)]
        qbT = [AR.alloc([128, 4, 128], BF16, f"qb{k}") for k in range(2)]
        Sst = [AR.alloc([128, 128], F32, f"S{k}") for k in range(4)]
        if ropeT.buf.w is None or tabstate["cur"] != 0:
            S.dma("sp", lambda e: e.dma_start(out=ropeT[:], in_=rope_in.rearrange("a p t -> p a t")), writes=[ropeT])
            tabstate["cur"] = 0
        WB2 = AR.alloc([128, KC, 768], BF16, "WB2")
        WBs = [WB, WB2]
        load_w("sp", WBs[0], WBs[0][:, :, 0:768], s_win0, 0, 768)
        for h in range(4):
            WBh = WBs[h % 2]
            if h + 1 < 4:
                load_w("sp", WBs[(h + 1) % 2], WBs[(h + 1) % 2][:, :, 0:768], s_win0, (h + 1) * 768, 768)
            rope_proj(WBh, 0, 128, qT, 0, 1, t1, t2)
            rope_proj(WBh, 256, 384, kT, 0, 1, t1, t2)
            for (t0, n) in SUP:
                pg = bank()
                projT(pg, WBh, 512, t0, n)
                S.op("act", lambda e, pg=pg, t0=t0, n=n: e.activation(out=gsT[:, t0:t0 + n], in_=pg[:, 0:n], func=AF.Silu), reads=[pg], writes=[gsT])
            for i0 in range(0, NT, 4):
                ni = min(4, NT - i0)
                pv = bank()
                for k in range(ni):
                    projTok(pv[:, k * 128:(k + 1) * 128], pv, WBh, 640, 128, i0 + k)
                S.op("act", lambda e, pv=pv, i0=i0, ni=ni: e.copy(out=Vt[:, i0:i0 + ni, :], in_=pv[:, 0:ni * 128].rearrange("p (a b) -> p a b", b=128)),
                     reads=[pv], writes=[Vt])
            if dbg.get("m0", 99) <= 1:
                return
            for i0 in range(0, NT, 8):
                ni = min(8, NT - i0)
                pt = tbank()
                for k in range(ni):
                    n_ = i0 + k
                    S.op("pe", lambda e, k=k, n_=n_, pt=pt: e.transpose(out=pt[:, k * 128:(k + 1) * 128], in_=kT[:, n_ * 128:(n_ + 1) * 128], identity=identB[:]),
                         reads=[kT, identB], writes=[pt])
                if dbg.get("kfmode", 0) == 1:
                    continue
                if dbg.get("kfmode", 0) == 2:
                    S.op("dve", lambda e, pt=pt, i0=i0, ni=ni, h=h: e.tensor_scalar(out=kf[:, i0:i0 + ni, :], in0=pt[:, 0:ni * 128].rearrange("p (a b) -> p a b", b=128),
                                                                                    scalar1=kdec[:, h:h + 1], scalar2=None, op0=ALU.mult), reads=[pt, kdec], writes=[kf])
                    S.op("dve", lambda e, pt=pt, i0=i0, ni=ni, h=h: e.tensor_scalar(out=kb_[:, i0:i0 + ni, :], in0=pt[:, 0:ni * 128].rearrange("p (a b) -> p a b", b=128),
                                                                                    scalar1=kdec[:, 4 + h:5 + h], scalar2=None, op0=ALU.mult), reads=[pt, kdec], writes=[kb_])
                    continue
                S.op("act", lambda e, pt=pt, i0=i0, ni=ni, h=h: e.activation(out=kf[:, i0:i0 + ni, :], in_=pt[:, 0:ni * 128].rearrange("p (a b) -> p a b", b=128),
                                                                       func=AF.Identity, scale=kdec[:, h:h + 1]), reads=[pt, kdec], writes=[kf])
                S.op("act", lambda e, pt=pt, i0=i0, ni=ni, h=h: e.activation(out=kb_[:, i0:i0 + ni, :], in_=pt[:, 0:ni * 128].rearrange("p (a b) -> p a b", b=128),
                                                                       func=AF.Identity, scale=kdec[:, 4 + h:5 + h]), reads=[pt, kdec], writes=[kb_])
            if dbg.get("m0", 99) <= 2:
                return
            has = {}
            chains = {"f": ([16, 17] + list(range(16)), kf, 8 + h, Sfb, Sst[0:2]),
                      "b": ([17, 16] + list(range(15, -1, -1)), kb_, 12 + h, Sbb, Sst[2:4])}
            KVd = {"f": KVf, "b": KVb}
            for tag in ("f", "b"):
                order, kd, cd, Sb_, Sp = chains[tag]
                KVt = KVd[tag]
                for g0 in range(0, NT, 4):
                    ng_ = min(4, NT - g0)
                    pk = bank()
                    for k in range(ng_):
                        n_ = g0 + k
                        S.op("pe", lambda e, pk=pk, kd=kd, n_=n_, k=k: e.matmul(pk[:, k * 128:(k + 1) * 128], lhsT=kd[:, n_, :], rhs=Vt[:, n_, :], start=True, stop=True),
                             reads=[kd, Vt], writes=[pk])
                    eng = "act" if tag == "f" else "dve"
                    if eng == "act":
                        S.op("act", lambda e, pk=pk, g0=g0, ng_=ng_, KVt=KVt: e.copy(out=KVt[:, g0:g0 + ng_, :], in_=pk[:, 0:ng_ * 128].rearrange("p (a b) -> p a b", b=128)),
                             reads=[pk], writes=[KVt])
                    else:
                        S.op("dve", lambda e, pk=pk, g0=g0, ng_=ng_, KVt=KVt: e.tensor_copy(out=KVt[:, g0:g0 + ng_, :], in_=pk[:, 0:ng_ * 128].rearrange("p (a b) -> p a b", b=128)),
                             reads=[pk], writes=[KVt])
            for idx in range(NT):
                for tag in ("f", "b"):
                    order, kd, cd, Sb_, Sp = chains[tag]
                    KVt = KVd[tag]
                    n_ = order[idx]
                    cur, nxt = Sp[idx % 2], Sp[(idx + 1) % 2]
                    has[(tag, n_)] = idx > 0
                    if idx > 0:
                        S.op("act", lambda e, cur=cur, Sb_=Sb_, n_=n_: e.copy(out=Sb_[:, n_, :], in_=cur[:]), reads=[cur], writes=[Sb_])
                    if idx < NT - 1:
                        if idx == 0:
                            S.op("dve", lambda e, nxt=nxt, n_=n_, KVt=KVt: e.tensor_copy(out=nxt[:], in_=KVt[:, n_, :]), reads=[KVt], writes=[nxt])
                        else:
                            S.op("dve", lambda e, cur=cur, nxt=nxt, cd=cd, n_=n_, KVt=KVt: e.scalar_tensor_tensor(out=nxt[:], in0=cur[:], scalar=kdec[:, cd:cd + 1], in1=KVt[:, n_, :],
                                                                                                          op0=ALU.mult, op1=ALU.add), reads=[KVt, cur, kdec], writes=[nxt])
            if dbg.get("m0", 99) <= 3:
                return
            tails = []
            for gi, (t0, n) in enumerate(SUP):
                ng = n // 128
                c0 = t0 // 128
                pi = bank()
                for k in range(ng):
                    cs = slice((c0 + k) * 128, (c0 + k + 1) * 128)
                    S.op("pe", lambda e, pi=pi, k=k, cs=cs: e.matmul(pi[:, k * 128:(k + 1) * 128], lhsT=kT[:, cs], rhs=qT[:, cs], start=True, stop=True),
                         reads=[kT, qT], writes=[pi])
                inn, qf_, qb_ = inner[gi % 2], qfT[gi % 2], qbT[gi % 2]
                S.op("dve", lambda e, pi=pi, inn=inn, ng=ng, h=h: e.tensor_tensor(out=inn[:, 0:ng, :], in0=pi[:, 0:ng * 128].rearrange("p (a b) -> p a b", b=128),
                                                                             in1=DTt[:, h, :].unsqueeze(1).to_broadcast([128, ng, 128]), op=ALU.mult),
                     reads=[pi, DTt], writes=[inn])
                S.op("pool", lambda e, qf_=qf_, ng=ng, t0=t0, n=n, h=h: e.tensor_tensor(out=qf_[:, 0:ng, :], in0=qT[:, t0:t0 + n].rearrange("p (a b) -> p a b", b=128),
                                                                                   in1=qdf[:, h, :].unsqueeze(1).to_broadcast([128, ng, 128]), op=ALU.mult),
                     reads=[qT, qdf], writes=[qf_])
                S.op("pool", lambda e, qb_=qb_, ng=ng, t0=t0, n=n, h=h: e.tensor_tensor(out=qb_[:, 0:ng, :], in0=qT[:, t0:t0 + n].rearrange("p (a b) -> p a b", b=128),
                                                                                   in1=qdb[:, h, :].unsqueeze(1).to_broadcast([128, ng, 128]), op=ALU.mult),
                     reads=[qT, qdb], writes=[qb_])
                po = bank()
                for k in range(ng):
                    n_ = c0 + k
                    terms = [(Vt[:, n_, :], inn[:, k, :], [Vt, inn])]
                    if has[("f", n_)]:
                        terms.append((Sfb[:, n_, :], qf_[:, k, :], [Sfb, qf_]))
                    if has[("b", n_)]:
                        terms.append((Sbb[:, n_, :], qb_[:, k, :], [Sbb, qb_]))
                    for ti, (lh, rh, rd) in enumerate(terms):
                        S.op("pe", lambda e, po=po, k=k, lh=lh, rh=rh, ti=ti, nt=len(terms): e.matmul(
                            po[:, k * 128:(k + 1) * 128], lhsT=lh, rhs=rh, start=(ti == 0), stop=(ti == nt - 1)), reads=rd, writes=[po])
                ysq_ = ysq2[gi % 2]
                S.op("act", lambda e, po=po, n=n, ysq_=ysq_: e.activation(out=ysq_[:, 0:n], in_=po[:, 0:n], func=AF.Square), reads=[po], writes=[ysq_])

                def ret_tail(po=po, n=n, t0=t0, ysq_=ysq_, h=h):
                    pn = bank()
                    S.op("pe", lambda e: e.matmul(pn[:, 0:n], lhsT=onesB[:], rhs=ysq_[:, 0:n], start=True, stop=True), reads=[onesB, ysq_], writes=[pn])
                    S.op("act", lambda e: e.activation(out=t3[:, 0:n], in_=pn[:, 0:n], func=AF.Sqrt, bias=EPS * 128.0, scale=1.0 / 128), reads=[pn], writes=[t3])
                    S.op("dve", lambda e: e.reciprocal(out=t3[:, 0:n], in_=t3[:, 0:n]), reads=[t3], writes=[t3])
                    S.op("dve", lambda e: e.tensor_tensor(out=t1[:, 0:n], in0=po[:, 0:n], in1=t3[:, 0:n], op=ALU.mult), reads=[po, t3], writes=[t1])
                    S.op("pool", lambda e: e.tensor_tensor(out=YT[:, h, t0:t0 + n], in0=t1[:, 0:n], in1=gsT[:, t0:t0 + n], op=ALU.mult),
                         reads=[t1, gsT], writes=[ybuf[i] for i in range(t0 // 128, (t0 + n) // 128)])

                tails.append(ret_tail)
                if len(tails) > 1:
                    tails.pop(0)()
            while tails:
                tails.pop(0)()
        if dbg.get("m0", 99) <= 4:
            return
        S.barrier()
        AR.reset()
        u1 = AR.alloc([128, 512], F32, "u1")
        u2 = AR.alloc([128, 512], F32, "u2")
        Q = [AR.alloc([128, TT], BF16, f"Q{k}") for k in range(2)]
        Kd = AR.alloc([128, TT], BF16, "Kd")
        KdM = [AR.alloc([128, TT], BF16, f"KdM{k}") for k in range(2)]
        S.op("pool", lambda e: e.memset(KdM[0][64:128, :], 0.0), writes=[KdM[0]])
        S.op("pool", lambda e: e.memset(KdM[1][0:64, :], 0.0), writes=[KdM[1]])
        Vz = AR.alloc([128, NT, 2, 128], BF16, "Vz")
        PTt = [AR.alloc([128, 5, 2, 128], BF16, f"PT{k}") for k in range(3)]
        PTb = [[Buf() for _ in range(5)] for k in range(3)]
        dn = [AR.alloc([128, 128], F32, f"dn{k}") for k in range(3)]
        WB3 = AR.alloc([128, KC, 832], BF16, "WB3")
        WBg = [WB, WB3]
        load_w("sp", WBg[0], WBg[0][:, :, 0:832], s_win0, 3072, 832)
        load_w("sp", WBg[1], WBg[1][:, :, 0:832], s_win0, 3072 + 832, 832)
        for g in range(2):
            WBc = WBg[g]
            rope_proj(WBc, 0, 128, Q[0], 2, 3, u1, u2)
            rope_proj(WBc, 256, 384, Q[1], 2, 3, u1, u2)
            rope_proj(WBc, 512, 640, Kd, 2, 3, u1, u2)
            S.op("act", lambda e: e.copy(out=KdM[0][0:64, :], in_=Kd[0:64, :]), reads=[Kd], writes=[KdM[0]])
            S.op("dve", lambda e: e.tensor_copy(out=KdM[1][64:128, :], in_=Kd[64:128, :]), reads=[Kd], writes=[KdM[1]])
            S.op("pool", lambda e: e.memset(Vz[:], 0.0), writes=[Vz])
            for i0 in range(0, NT, 8):
                ni = min(8, NT - i0)
                pv = bank()
                for k in range(ni):
                    projTok(pv[:, k * 64:(k + 1) * 64], pv, WBc, 768, 64, i0 + k)
                S.op("act", lambda e, pv=pv, i0=i0, ni=ni: e.copy(out=Vz[:, i0:i0 + ni, 0, 0:64], in_=pv[:, 0:ni * 64].rearrange("p (a b) -> p a b", b=64)),
                     reads=[pv], writes=[Vz])
                S.op("dve", lambda e, pv=pv, i0=i0, ni=ni: e.tensor_copy(out=Vz[:, i0:i0 + ni, 1, 64:128], in_=pv[:, 0:ni * 64].rearrange("p (a b) -> p a b", b=64)),
                     reads=[pv], writes=[Vz])
            cnt = 0
            if dbg.get("swa", 99) <= 1:
                return
            def swa_a(it, qc, Pt, Pb):
                if it < 16:
                    kts = [(kt, (0 if kt < it else (1 if kt > it else None))) for kt in (it - 1, it, it + 1) if 0 <= kt < 16] + [(16, None), (17, None)]
                else:
                    kts = [(16, None), (17, None)]
                qs = slice(it * 128, (it + 1) * 128)
                for b0 in range(0, len(kts), 2):
                    nb = min(2, len(kts) - b0)
                    ps = bank()
                    for k in range(nb):
                        kt = kts[b0 + k][0]
                        ks = slice(kt * 128, (kt + 1) * 128)
                        for hh in range(2):
                            S.op("pe", lambda e, ps=ps, k=k, hh=hh, ks=ks, qs=qs, qc=qc: e.matmul(
                                ps[:, (k * 2 + hh) * 128:(k * 2 + hh + 1) * 128], lhsT=KdM[hh][:, ks], rhs=Q[qc][:, qs], start=True, stop=True),
                                reads=[KdM[hh], Q[qc]], writes=[ps])
                    S.op("act", lambda e, ps=ps, Pt=Pt, b0=b0, nb=nb: e.activation(
                        out=Pt[:, b0:b0 + nb, :, :].rearrange("p a h b -> p (a h b)"), in_=ps[:, 0:nb * 256], func=AF.Exp, scale=0.125),
                        reads=[ps], writes=[Pb[b0 + k] for k in range(nb)])
                for k, (kt, mk) in enumerate(kts):
                    if mk is not None:
                        S.op("pool", lambda e, Pt=Pt, k=k, mk=mk: e.tensor_tensor(out=Pt[:, k, :, :], in0=Pt[:, k, :, :],
                                                                                in1=swaM[:, mk, :].unsqueeze(1).to_broadcast([128, 2, 128]), op=ALU.mult),
                             reads=[Pb[k], swaM], writes=[Pb[k]])
                return kts

            def swa_b(it, qc, Pt, Pb, dnt, kts):
                qs = slice(it * 128, (it + 1) * 128)
                po = bank()
                nk = len(kts)
                for k, (kt, mk) in enumerate(kts):
                    for hh in range(2):
                        S.op("pe", lambda e, po=po, Pt=Pt, k=k, kt=kt, hh=hh, nk=nk: e.matmul(
                            po[:, 0:128], lhsT=Vz[:, kt, hh, :], rhs=Pt[:, k, hh, :], start=(k == 0 and hh == 0), stop=(k == nk - 1 and hh == 1)),
                            reads=[Vz, Pb[k]], writes=[po])
                for k, (kt, mk) in enumerate(kts):
                    for hh in range(2):
                        sel = selL if hh == 0 else selR
                        S.op("pe", lambda e, po=po, Pt=Pt, k=k, hh=hh, nk=nk, sel=sel: e.matmul(
                            po[:, 128:256], lhsT=sel[:], rhs=Pt[:, k, hh, :], start=(k == 0 and hh == 0), stop=(k == nk - 1 and hh == 1)),
                            reads=[sel, Pb[k]], writes=[po])
                col = 2 * g + qc
                S.op("dve", lambda e, po=po, dnt=dnt, col=col: e.tensor_scalar(out=dnt[:], in0=po[:, 128:256], scalar1=esink[:, col:col + 1], scalar2=None, op0=ALU.add),
                     reads=[po, esink], writes=[dnt])
                S.op("dve", lambda e, dnt=dnt: e.reciprocal(out=dnt[:], in_=dnt[:]), reads=[dnt], writes=[dnt])
                S.op("dve", lambda e, po=po, dnt=dnt, col=col, qs=qs: e.tensor_tensor(out=YT[:, 4 + col, qs], in0=po[:, 0:128], in1=dnt[:], op=ALU.mult),
                     reads=[po, dnt], writes=[ybuf[it]])

            pend = []
            for it in range(NT):
                for qc in range(2):
                    Pt, Pb, dnt = PTt[cnt % 3], PTb[cnt % 3], dn[cnt % 3]
                    cnt += 1
                    kts = swa_a(it, qc, Pt, Pb)
                    pend.append((it, qc, Pt, Pb, dnt, kts))
                    if len(pend) > 2:
                        swa_b(*pend.pop(0))
            while pend:
                swa_b(*pend.pop(0))

    tabstate = {"cur": -1}

    def mix1_phase(s):
        S.barrier()
        AR.reset()
        sets = []
        for b_ in range(2):
            Qh_ = AR.alloc([128, T], BF16, f"Qh{b_}")
            KhM_ = [AR.alloc([128, TT], BF16, f"KhM{b_}{k}") for k in range(2)]
            Vz_ = AR.alloc([128, NT, 2, 128], BF16, f"Vz{b_}")
            S.op("pool", lambda e, KhM_=KhM_: e.memset(KhM_[0][64:128, :], 0.0), writes=[KhM_[0]])
            S.op("pool", lambda e, KhM_=KhM_: e.memset(KhM_[1][0:64, :], 0.0), writes=[KhM_[1]])
            S.op("pool", lambda e, Vz_=Vz_: e.memset(Vz_[:], 0.0), writes=[Vz_])
            sets.append((Qh_, KhM_, Vz_))
        PTt = [AR.alloc([128, 7, 2, 128], BF16, f"PT{k}") for k in range(3)]
        PTb = [[Buf() for _ in range(7)] for k in range(3)]
        lt = [AR.alloc([128, 2, 128], F32, f"lt{k}") for k in range(4)]
        dn = [AR.alloc([128, 128], F32, f"dn{k}") for k in range(3)]
        NA_DEPTH = dbg.get("na_depth", 2)
        if tabstate["cur"] != 1:
            S.dma("pool", lambda e: e.dma_start(out=rpbT[:], in_=rpbg_in.rearrange("p (o c x) -> p o c x", o=7, c=8)), writes=[rpbT])
            S.dma("pool", lambda e: e.dma_start(out=namT[:], in_=nam_in), writes=[namT])
            tabstate["cur"] = 1
        WB4 = AR.alloc([128, KC, 384], BF16, "WB4")
        WBn = [WB, WB4]

        def proj_units(WBc, Qh, KhM, Vz):
            units = []
            for (t0, n) in SUP[:4]:
                def uq(t0=t0, n=n):
                    pq = bank()
                    projT(pq, WBc, 0, t0, n)
                    S.op("act", lambda e: e.copy(out=Qh[:, t0:t0 + n], in_=pq[:, 0:n]), reads=[pq], writes=[Qh])
                units.append(uq)
            for (t0, n) in SUP:
                def uk(t0=t0, n=n):
                    pk = bank()
                    projT(pk, WBc, 128, t0, n)
                    S.op("dve", lambda e: e.tensor_copy(out=KhM[1][64:128, t0:t0 + n], in_=pk[64:128, 0:n]), reads=[pk], writes=[KhM[1]])
                    S.op("act", lambda e: e.copy(out=KhM[0][0:64, t0:t0 + n], in_=pk[0:64, 0:n]), reads=[pk], writes=[KhM[0]])
                units.append(uk)
            for i0 in range(0, NT, 4):
                def uv(i0=i0):
                    ni = min(4, NT - i0)
                    pv = bank()
                    for k in range(ni):
                        projTok(pv[:, k * 128:(k + 1) * 128], pv, WBc, 256, 128, i0 + k)
                    pvv = pv[:, 0:ni * 128].rearrange("p (a b) -> p a b", b=128)
                    S.op("act", lambda e: e.copy(out=Vz[:, i0:i0 + ni, 0, 0:64], in_=pvv[:, :, 0:64]), reads=[pv], writes=[Vz])
                    S.op("dve", lambda e: e.tensor_copy(out=Vz[:, i0:i0 + ni, 1, 64:128], in_=pvv[:, :, 64:128]), reads=[pv], writes=[Vz])
                units.append(uv)
            return units

        def na_a(hc, Qh, KhM, it, Pt, Pb):
            kts = [(a, o_, pid) for (a, o_, pid) in na_plan[it]] + [(16, None, None), (17, None, None)]
            qs = slice(it * 128, (it + 1) * 128)
            for k, (kt, o_, pid) in enumerate(kts):
                ks = slice(kt * 128, (kt + 1) * 128)
                if o_ is not None:
                    ps = bank()
                    for hh in range(2):
                        S.op("pe", lambda e, ps=ps, hh=hh, ks=ks: e.matmul(ps[:, hh * 128:(hh + 1) * 128], lhsT=KhM[hh][:, ks], rhs=Qh[:, qs], start=True, stop=True),
                             reads=[KhM[hh], Qh], writes=[ps])
                    ltt = lt[k % 4]
                    S.op("dve", lambda e, ps=ps, ltt=ltt, o_=o_: e.scalar_tensor_tensor(
                        out=ltt[:], in0=ps[:, 0:256].rearrange("p (h b) -> p h b", b=128), scalar=0.125,
                        in1=rpbT[:, o_ + 3, hc, :].rearrange("p (h b) -> p h b", b=128), op0=ALU.mult, op1=ALU.add), reads=[ps, rpbT], writes=[ltt])
                    S.op("act", lambda e, ltt=ltt, k=k: e.activation(out=Pt[:, k, :, :], in_=ltt[:], func=AF.Exp), reads=[ltt], writes=[Pb[k]])
                    S.op("pool", lambda e, k=k, pid=pid: e.tensor_tensor(out=Pt[:, k, :, :], in0=Pt[:, k, :, :],
                                                                      in1=namT[:, pid, :].unsqueeze(1).to_broadcast([128, 2, 128]), op=ALU.mult),
                         reads=[Pb[k], namT], writes=[Pb[k]])
            ps = bank()
            kc0 = len(kts) - 2
            for k2 in range(2):
                ks = slice((16 + k2) * 128, (17 + k2) * 128)
                for hh in range(2):
                    S.op("pe", lambda e, ps=ps, k2=k2, hh=hh, ks=ks: e.matmul(
                        ps[:, (k2 * 2 + hh) * 128:(k2 * 2 + hh + 1) * 128], lhsT=KhM[hh][:, ks], rhs=Qh[:, qs], start=True, stop=True), reads=[KhM[hh], Qh], writes=[ps])
            S.op("act", lambda e, ps=ps: e.activation(out=Pt[:, kc0:kc0 + 2, :, :].rearrange("p a h b -> p (a h b)"), in_=ps[:, 0:512],
                                                      func=AF.Exp, scale=0.125), reads=[ps], writes=[Pb[kc0], Pb[kc0 + 1]])
            return kts

        def na_b(hc, Vz, it, Pt, Pb, dnt, kts):
            qs = slice(it * 128, (it + 1) * 128)
            po = bank()
            nk = len(kts)
            for k, (kt, o_, pid) in enumerate(kts):
                for hh in range(2):
                    S.op("pe", lambda e, k=k, kt=kt, hh=hh: e.matmul(
                        po[:, 0:128], lhsT=Vz[:, kt, hh, :], rhs=Pt[:, k, hh, :], start=(k == 0 and hh == 0), stop=(k == nk - 1 and hh == 1)),
                        reads=[Vz, Pb[k]], writes=[po])
            for k, (kt, o_, pid) in enumerate(kts):
                for hh in range(2):
                    sel = selL if hh == 0 else selR
                    S.op("pe", lambda e, k=k, hh=hh, sel=sel: e.matmul(
                        po[:, 128:256], lhsT=sel[:], rhs=Pt[:, k, hh, :], start=(k == 0 and hh == 0), stop=(k == nk - 1 and hh == 1)),
                        reads=[sel, Pb[k]], writes=[po])
            S.op("dve", lambda e: e.reciprocal(out=dnt[:], in_=po[:, 128:256]), reads=[po], writes=[dnt])
            S.op("dve", lambda e: e.tensor_tensor(out=YT[:, hc, qs], in0=po[:, 0:128], in1=dnt[:], op=ALU.mult),
                 reads=[po, dnt], writes=[ybuf[it]])

        load_w("sp", WBn[0], WBn[0][:, :, 0:384], s_win1, 0, 384)
        for u in proj_units(WBn[0], *sets[0]):
            u()
        cnt = 0
        for hc in range(8):
            Qh, KhM, Vz = sets[hc % 2]
            nxt = []
            if hc + 1 < 8:
                load_w("sp", WBn[(hc + 1) % 2], WBn[(hc + 1) % 2][:, :, 0:384], s_win1, (hc + 1) * 384, 384)
                nxt = proj_units(WBn[(hc + 1) % 2], *sets[(hc + 1) % 2])
            pend = []
            for it in range(16):
                Pt, Pb, dnt = PTt[cnt % 3], PTb[cnt % 3], dn[cnt % 3]
                cnt += 1
                kts = na_a(hc, Qh, KhM, it, Pt, Pb)
                pend.append((hc, Vz, it, Pt, Pb, dnt, kts))
                if len(pend) > NA_DEPTH:
                    na_b(*pend.pop(0))
                if nxt and it >= 1:
                    nxt.pop(0)()
            while pend:
                na_b(*pend.pop(0))
            while nxt:
                nxt.pop(0)()

    def residual(s, i, psA, psB, GG, X, Xn, T1, ss):
        S.op("act", lambda e: e.activation(out=T1[:, 0:512], in_=psA[:, :], func=AF.Square, accum_out=ss[:, 0:1]), reads=[psA], writes=[T1, ss])
        S.op("act", lambda e: e.activation(out=T1[:, 512:1024], in_=psB[:, :], func=AF.Square, accum_out=ss[:, 1:2]), reads=[psB], writes=[T1, ss])
        S.op("dve", lambda e: e.tensor_tensor(out=ss[:, 0:1], in0=ss[:, 0:1], in1=ss[:, 1:2], op=ALU.add), reads=[ss], writes=[ss])
        S.op("act", lambda e: e.activation(out=ss[:, 1:2], in_=ss[:, 0:1], func=AF.Sqrt, bias=EPS, scale=1.0 / D), reads=[ss], writes=[ss])
        S.op("dve", lambda e: e.reciprocal(out=ss[:, 2:3], in_=ss[:, 1:2]), reads=[ss], writes=[ss])
        S.op("dve", lambda e: e.tensor_tensor(out=T1[:, 0:512], in0=psA[:, :], in1=GG[:, 0:512], op=ALU.mult), reads=[psA, GG], writes=[T1])
        S.op("dve", lambda e: e.tensor_tensor(out=T1[:, 512:1024], in0=psB[:, :], in1=GG[:, 512:1024], op=ALU.mult), reads=[psB, GG], writes=[T1])
        S.op("dve", lambda e: e.scalar_tensor_tensor(out=Xn[:], in0=T1[:], scalar=ss[:, 2:3], in1=X[:], op0=ALU.mult, op1=ALU.add),
             reads=[T1, ss, X], writes=[Xn])
        S.dma("sp", lambda e: e.dma_start(out=x_dst(s, i), in_=Xn[:]), reads=[Xn], writes=[xb(s, i)])

    def load_gg(GG, l, wi, b):
        S.dma("sp", lambda e: e.dma_start(out=GG[:], in_=ggd[l, wi, b].partition_broadcast(128)), reads=[ggbuf[(l, wi, b)]], writes=[GG])

    def o1_phase(l, s, first, ntiles):
        S.barrier()
        AR.reset()
        Wo = AR.alloc([128, KC, D], BF16, "Wo")
        GGx = AR.alloc([128, D], F32, "GGx")
        GGc = AR.alloc([128, D], F32, "GGc")
        NBUF = 3
        Xs = [AR.alloc([128, D], F32, f"X{k}") for k in range(NBUF)]
        T1 = [AR.alloc([128, D], F32, f"T{k}") for k in range(NBUF)]
        junk = AR.alloc([128, D], BF16, "junk")
        sc = [dict(ss=AR.alloc([128, 4], F32, f"ss{k}"), junk=junk, xs=AR.alloc([128, D], BF16, f"xs{k}"), tmp=AR.alloc([128, D], F32, f"tmp{k}"))
              for k in range(NBUF)]
        ssr = [AR.alloc([128, 4], F32, f"ssr{k}") for k in range(NBUF)]
        load_w("sp", Wo, Wo[:], s_wout0 if l == 0 else s_wout1, 0, D)
        load_gg(GGx, l, 0, s)
        if ntiles > 16:
            load_gg(GGc, l, 0, NB - 1)

        def st1(i):
            X = Xs[i % NBUF]
            S.dma("sp", lambda e, X=X, i=i: e.dma_start(out=X[:], in_=x_ap(l, s, i, first)), reads=[xb(s, i)], writes=[X])
            pa, pb_ = bank(), bank()
            for hf, ps in enumerate((pa, pb_)):
                for kc in range(KC):
                    S.op("pe", lambda e, ps=ps, kc=kc, hf=hf, i=i: e.matmul(ps[:, :], lhsT=YT[:, kc, i * 128:(i + 1) * 128], rhs=Wo[:, kc, hf * 512:(hf + 1) * 512],
                                                                           start=(kc == 0), stop=(kc == KC - 1)), reads=[ybuf[i], Wo], writes=[ps])
            return (i, pa, pb_)

        def st2(i, pa, pb_):
            k = i % NBUF
            residual(s, i, pa, pb_, GGx if i < 16 else GGc, Xs[k], T1[k], T1[k], ssr[k])
            pt = norm_a(T1[k], i, sc[k])
            b = s if i < 16 else NB - 1
            return (pt, i, DER["A2"][:, b, :], DER["B2"][:, b, :], sc[k])

        q1, q2 = [], []
        for i in range(ntiles):
            q1.append(st1(i))
            if len(q1) > 1:
                q2.append(st2(*q1.pop(0)))
            if len(q2) > 1:
                norm_b(*q2.pop(0))
        while q1:
            q2.append(st2(*q1.pop(0)))
            if len(q2) > 1:
                norm_b(*q2.pop(0))
        while q2:
            norm_b(*q2.pop(0))

    def ffn_phase(l, s, ntiles):
        S.barrier()
        AR.reset()
        mT = AR.alloc([128, NJ, 512], BF16, "mT")
        mTb = [Buf(f"mT{j}") for j in range(NJ)]
        Wu = [[AR.alloc([128, KC, 256], BF16, f"wu{a}{k}") for k in range(2)] + [WUX[a]] for a in range(2)]
        GGx = AR.alloc([128, D], F32, "GGx")
        Xs = [AR.alloc([128, D], F32, f"X{k}") for k in range(2)]
        Xn = Xs
        T1 = [AR.alloc([128, D], F32, f"T{k}") for k in range(2)]
        ssr = [AR.alloc([128, 4], F32, f"ssr{k}") for k in range(2)]
        ua = [AR.alloc([128, 2, 256], F32, f"ua{k}") for k in range(2)]
        ug = [AR.alloc([128, 2, 256], F32, f"ug{k}") for k in range(2)]
        sg = [AR.alloc([128, 2, 256], BF16, f"sg{k}") for k in range(2)]
        load_gg(GGx, l, 1, s)
        S.dma("act", lambda e: e.dma_start(out=WD[:], in_=s_wdown[l].ap.rearrange("(j p) n -> p j n", p=128)), reads=s_wdown[l].bufs, writes=[WD])
        sups = SUP[:4] + ([SUP[4]] if ntiles > 16 else [])
        cnt = 0
        nblk = len(sups) * 11

        def wload(q):
            if q >= nblk:
                return
            jb = q % 11
            wa, wg = Wu[0][q % 3], Wu[1][q % 3]
            S.dma("sp", lambda e, wa=wa, jb=jb: e.dma_start(out=wa[:], in_=wups[l][jb].rearrange("p (kc n) -> p kc n", kc=KC)),
                  reads=[s_wup[l].bufs[jb]], writes=[wa])
            S.dma("sp", lambda e, wg=wg, jb=jb: e.dma_start(out=wg[:], in_=wups[l][11 + jb].rearrange("p (kc n) -> p kc n", kc=KC)),
                  reads=[s_wup[l].bufs[11 + jb]], writes=[wg])

        wload(0)
        wload(1)
        for si, (s0, sn) in enumerate(sups):
            nh = sn // 256
            if s0 >= T:
                load_gg(GGx, l, 1, NB - 1)
            for jb in range(11):
                q = si * 11 + jb
                wa, wg = Wu[0][q % 3], Wu[1][q % 3]
                wload(q + 2)
                for cc in range(2):
                    j = jb * 2 + cc
                    pa, pg = pbank(), pbank()
                    for hf in range(nh):
                        t0 = s0 + hf * 256
                        c0 = hcol(t0) - 1
                        tiles = [hbuf[i] for i in range(max(t0 // 128 - 1, 0), min(t0 // 128 + 3, NT))]
                        for (pp, w) in ((pa, wa), (pg, wg)):
                            for kc in range(KC):
                                S.op("pe", lambda e, pp=pp, w=w, kc=kc, hf=hf, c0=c0, cc=cc: e.matmul(
                                    pp[:, hf * 512:hf * 512 + 258], lhsT=w[:, kc, cc * 128:(cc + 1) * 128], rhs=hT[:, kc, c0:c0 + 258],
                                    start=(kc == 0), stop=(kc == KC - 1)), reads=[w] + tiles, writes=[pp])
                    k = cnt % 2
                    cnt += 1

                    def pv(pp, o_):
                        return pp[:, :].rearrange("p (a b) -> p a b", a=2)[:, 0:nh, o_:o_ + 256]

                    paths = ((pa, ua[k], j), (pg, ug[k], NJ + j))
                    for (pp, u, jj) in paths:
                        S.op("act", lambda e, pp=pp, u=u, jj=jj, nh=nh: e.activation(out=u[:, 0:nh, :], in_=pp[:, :].rearrange("p (a b) -> p a b", a=2)[:, 0:nh, 1:257],
                                                                                   func=AF.Identity, bias=cbT[:, jj:jj + 1], scale=cwT[:, 1, jj:jj + 1]),
                             reads=[pp, cbT, cwT], writes=[u])
                    for tap, o_ in ((0, 0), (2, 2)):
                        for (pp, u, jj) in paths:
                            S.op("dve", lambda e, pp=pp, u=u, jj=jj, nh=nh, tap=tap, o_=o_: e.scalar_tensor_tensor(
                                out=u[:, 0:nh, :], in0=pp[:, :].rearrange("p (a b) -> p a b", a=2)[:, 0:nh, o_:o_ + 256], scalar=cwT[:, tap, jj:jj + 1],
                                in1=u[:, 0:nh, :], op0=ALU.mult, op1=ALU.add), reads=[pp, cwT, u], writes=[u])
                    S.op("act", lambda e, k=k, nh=nh: e.activation(out=sg[k][:, 0:nh, :], in_=ug[k][:, 0:nh, :], func=AF.Silu), reads=[ug[k]], writes=[sg[k]])
                    S.op("pool", lambda e, k=k, j=j, nh=nh, sn=sn: e.tensor_tensor(out=mT[:, j, 0:sn].rearrange("p (a b) -> p a b", b=256), in0=sg[k][:, 0:nh, :],
                                                                                 in1=ua[k][:, 0:nh, :], op=ALU.mult), reads=[sg[k], ua[k]], writes=[mTb[j]])
            for it in range(sn // 128):
                i = s0 // 128 + it
                X = Xs[i % 2]
                S.dma("sp", lambda e, X=X, i=i: e.dma_start(out=X[:], in_=x_ap(l, s, i, False)), reads=[xb(s, i)], writes=[X])
                pp = pbank()
                pa = PView(pp.base[:, 0:512], pp.buf)
                pb_ = PView(pp.base[:, 512:1024], pp.buf)
                for hf, ps in enumerate((pa, pb_)):
                    for j in range(NJ):
                        S.op("pe", lambda e, ps=ps, j=j, hf=hf, it=it: e.matmul(ps[:, :], lhsT=mT[:, j, it * 128:(it + 1) * 128], rhs=WD[:, j, hf * 512:(hf + 1) * 512],
                                                                                start=(j == 0), stop=(j == NJ - 1)), reads=[mTb[j], WD], writes=[ps])
                residual(s, i, pa, pb_, GGx, X, Xn[i % 2], T1[i % 2], ssr[i % 2])

    stop = dbg.get("stop", 99)
    for l in layers:
        if stop <= 1:
            break
        mod_phase(l)
        if stop <= 2:
            break
        last = l == 1
        nt = 16 if last else NT
        for s in range(NSEQ):
            n1_phase(l, s, True)
            if l == first_layer:
                nd = (len(deferred) + NSEQ - 1 - s) // (NSEQ - s) if deferred else 0
                for _ in range(nd):
                    stage_layer(*deferred.pop(0))
            if stop <= 3:
                break
            if l == 0:
                mix0_phase(s)
            else:
                mix1_phase(s)
            if stop <= 5:
                break
            if dbg.get("yt") == l and s == 0:
                S.barrier()
                AR.reset()
                dy = dbg_out("yt", [128, KC * TT])
                ytf = AR.alloc([128, KC * TT // 2], F32, "ytf")
                for hf in range(2):
                    S.op("dve", lambda e, hf=hf: e.tensor_copy(out=ytf[:], in_=YT[:].rearrange("p c t -> p (c t)")[:, hf * KC * TT // 2:(hf + 1) * KC * TT // 2]),
                         reads=ybuf, writes=[ytf])
                    S.dma("sp", lambda e, hf=hf: e.dma_start(out=dy[:, hf * KC * TT // 2:(hf + 1) * KC * TT // 2], in_=ytf[:]), reads=[ytf])
            o1_phase(l, s, True, nt)
            if stop <= 6:
                break
            ffn_phase(l, s, nt)
    S.barrier(final=True)
    S.emit()
    return nc, S


_CACHE = {}


def _host_layout(inputs, NSEQ, core):
    f = lambda a: np.ascontiguousarray(np.asarray(a, dtype=np.float32))
    b0 = core * NSEQ
    c = np.concatenate([np.asarray(inputs["c"])[b0:b0 + NSEQ], np.asarray(inputs["c_ctx"])[None, :]], axis=0)
    NB = NSEQ + 1
    m = {
        "x": f(np.asarray(inputs["x"])[b0:b0 + NSEQ]),
        "ctx": f(np.asarray(inputs["ctx"])[b0:b0 + NSEQ]),
        "cT": f(c.reshape(NB, KC, 128).transpose(2, 1, 0)),
    }
    return m


def _shared_layout(inputs):
    f = lambda a: np.ascontiguousarray(np.asarray(a, dtype=np.float32))
    key = "shared"
    plan, pats = _na_geometry()
    ret_tab, ret_cols = _ret_tables()
    m = {
        "w_mod": f(inputs["w_mod"]),
        "b_modT": f(np.asarray(inputs["b_mod"]).reshape(2, 48, 128).transpose(0, 2, 1)),
        "norm_gT": f(np.asarray(inputs["norm_g"]).reshape(2, 4, 8, 128).transpose(0, 3, 1, 2)),
        "ab_w_in": f(np.asarray(inputs["ab_w_in"])[0]),
        "ab_w_out": f(np.asarray(inputs["ab_w_out"])[0]),
        "ret_decay": f(np.asarray(inputs["ret_decay_exp"]).reshape(1, 8)),
        "swa_sink": f(np.asarray(inputs["swa_sink"]).reshape(1, 8)),
        "na_w_in": f(np.asarray(inputs["na_w_in"])[0]),
        "na_w_out": f(np.asarray(inputs["na_w_out"])[0]),
        "rpb_g": f(_rpb_gather(np.asarray(inputs["na_rpb"], dtype=np.float32)[0]).reshape(128, -1)),
        "ffn_w_up": f(inputs["ffn_w_up"]),
        "conv_wT": f(np.asarray(inputs["ffn_conv_w"]).reshape(2, 3, 44, 128).transpose(0, 3, 1, 2)),
        "conv_bT": f(np.asarray(inputs["ffn_conv_b"]).reshape(2, 44, 128).transpose(0, 2, 1)),
        "ffn_w_down": f(inputs["ffn_w_down"]),
        "rope_tab": _rope_tables(),
        "ret_tab": f(ret_tab),
        "ret_cols": f(ret_cols),
        "swa_mask": f(_swa_masks()),
        "na_mask": f(pats.transpose(1, 0, 2)),
    }
    return m, plan, pats.shape[0]


def kernel(**inputs):
    n_cores = 8
    B = np.asarray(inputs["x"]).shape[0]
    NSEQ = B // n_cores
    shared, plan, npat = _shared_layout(inputs)
    if "nc" not in _CACHE:
        _CACHE["nc"] = build_program(NSEQ, plan, npat)[0]
    nc = _CACHE["nc"]
    in_maps = []
    for core in range(n_cores):
        m = dict(shared)
        m.update(_host_layout(inputs, NSEQ, core))
        in_maps.append(m)
    res = run_bass_kernel_spmd(nc, in_maps, core_ids=list(range(n_cores)))
    return np.concatenate([np.asarray(r["out"], dtype=np.float32) for r in res.results], axis=0)
```
